# Optimizing a Trainium2 kernel written in Bass

```python
import math, functools
import jax, jax.numpy as jnp
from jax import lax
import numpy as np

D_MODEL = 2048
BATCH = 8
SEQ = 2048
DEPTH = 1
DEC_BATCH = 128
DEC_SEQ = 8
PAST_LEN = 8192
PAGE_SIZE = 128

HEAD_DIM = 64
N_Q_HEADS = D_MODEL // HEAD_DIM
N_KV_HEADS = N_Q_HEADS // 8
GQA_GROUP = N_Q_HEADS // N_KV_HEADS
WINDOW = 128
D_Q = N_Q_HEADS * HEAD_DIM
D_KV = N_KV_HEADS * HEAD_DIM
POOL_WINDOWS = (2, 4, 8, 16)
N_POOL_GROUPS = len(POOL_WINDOWS)
D_POOL = D_MODEL // 2
POOL_GROUP = D_POOL // N_POOL_GROUPS
POOL_OUT_GROUP = D_MODEL // N_POOL_GROUPS
POOL_HIST = max(POOL_WINDOWS) - 1
D_IN = D_Q + 2 * D_KV + D_POOL + 2 * D_MODEL
D_FF = 4 * D_MODEL
EPS = 1e-6
NEG_INF = -1e30

kernel_name = "hybrid_swa_sink_pool_gated_decoder_step"


def rmsnorm(x, g):
    xf = x.astype(jnp.float32)
    r = lax.rsqrt(jnp.mean(xf * xf, axis=-1, keepdims=True) + EPS)
    return (xf * r * g.astype(jnp.float32)).astype(x.dtype)


def split_proj(h, w_in):
    B, T, _ = h.shape
    z = jnp.einsum('btd,de->bte', h, w_in)
    c = np.cumsum([D_Q, D_KV, D_KV, D_POOL, D_MODEL])
    q, k, v, u, ga, gp = jnp.split(z, list(c), axis=-1)
    q = q.reshape(B, T, N_KV_HEADS, GQA_GROUP, HEAD_DIM)
    k = k.reshape(B, T, N_KV_HEADS, HEAD_DIM)
    v = v.reshape(B, T, N_KV_HEADS, HEAD_DIM)
    return q, k, v, u, ga, gp


def sink_softmax(s, mask, sinks):
    s = jnp.where(mask, s, NEG_INF)
    sk = sinks.astype(jnp.float32)[:, :, None]
    m = jnp.maximum(jnp.max(s, axis=-1), sk)
    p = jnp.exp(s - m[..., None])
    denom = jnp.sum(p, axis=-1) + jnp.exp(sk - m)
    return p / denom[..., None]


def window_attn_prompt(q, k, v, sinks):
    B, S = q.shape[:2]
    nb = S // WINDOW
    scale = HEAD_DIM ** -0.5
    qb = q.reshape(B, nb, WINDOW, N_KV_HEADS, GQA_GROUP, HEAD_DIM)
    kc = k.reshape(B, nb, WINDOW, N_KV_HEADS, HEAD_DIM)
    vc = v.reshape(B, nb, WINDOW, N_KV_HEADS, HEAD_DIM)
    kb = jnp.concatenate([jnp.concatenate([jnp.zeros_like(kc[:, :1]), kc[:, :-1]], 1), kc], axis=2)
    vb = jnp.concatenate([jnp.concatenate([jnp.zeros_like(vc[:, :1]), vc[:, :-1]], 1), vc], axis=2)
    s = jnp.einsum('bnqhgd,bnkhd->bnhgqk', qb.astype(jnp.float32), kb.astype(jnp.float32)) * scale
    i = jnp.arange(WINDOW)[:, None] + WINDOW
    j = jnp.arange(2 * WINDOW)[None, :]
    rel = i - j
    band = (rel >= 0) & (rel < WINDOW)
    has_prev = (jnp.arange(nb)[:, None, None] > 0) | (j >= WINDOW)[None]
    mask = (band[None] & has_prev)[None, :, None, None]
    probs = sink_softmax(s, mask, sinks)
    out = jnp.einsum('bnhgqk,bnkhd->bnqhgd', probs, vb.astype(jnp.float32))
    wb = min(WINDOW, S)
    return out.reshape(B, S, D_Q).astype(q.dtype), k[:, S - wb:], v[:, S - wb:]


def window_attn_sample(q, k, v, k_hist, v_hist, sinks):
    B, T = q.shape[:2]
    wb = k_hist.shape[1]
    scale = HEAD_DIM ** -0.5
    k_ext = jnp.concatenate([k_hist, k.astype(k_hist.dtype)], axis=1)
    v_ext = jnp.concatenate([v_hist, v.astype(v_hist.dtype)], axis=1)
    s = jnp.einsum('bqhgd,bkhd->bhgqk', q.astype(jnp.float32), k_ext.astype(jnp.float32)) * scale
    rel = (wb + jnp.arange(T))[:, None] - jnp.arange(wb + T)[None, :]
    mask = ((rel >= 0) & (rel < WINDOW))[None, None, None]
    probs = sink_softmax(s, mask, sinks)
    out = jnp.einsum('bhgqk,bkhd->bqhgd', probs, v_ext.astype(jnp.float32))
    return out.reshape(B, T, D_Q).astype(q.dtype), k_ext[:, T:], v_ext[:, T:]


def pool_branch(u_hist, u, pos0, w_pool, pool_scale):
    B, T, _ = u.shape
    P = u_hist.shape[1]
    u_ext = jnp.concatenate([u_hist.astype(u.dtype), u], axis=1)
    uf = u_ext.astype(jnp.float32)
    cs = jnp.concatenate([jnp.zeros((B, 1, D_POOL), jnp.float32), jnp.cumsum(uf, axis=1)], axis=1)
    end = P + jnp.arange(T) + 1
    pos = pos0 + jnp.arange(T)
    u_new = uf[:, P:]
    outs = []
    for g, w in enumerate(POOL_WINDOWS):
        sl = slice(g * POOL_GROUP, (g + 1) * POOL_GROUP)
        wsum = cs[:, end, sl] - cs[:, end - w, sl]
        cnt = jnp.minimum(pos + 1, w).astype(jnp.float32)[None, :, None]
        pooled = wsum / cnt - u_new[:, :, sl]
        outs.append(jnp.einsum('btc,cd->btd', pooled, w_pool[g].astype(jnp.float32)))
    p = jnp.concatenate(outs, axis=-1) * pool_scale.astype(jnp.float32)
    return p.astype(u.dtype), u_ext[:, P + T - POOL_HIST:]


def decoder_layer(x, attn_fn, u_hist, pos0, norm_attn_pre, norm_attn_post, w_in, w_pool,
                  pool_scale, w_out, norm_mlp_pre, norm_mlp_post, w_up, w_down):
    h = rmsnorm(x, norm_attn_pre)
    q, k, v, u, ga, gp = split_proj(h, w_in)
    a, k_state, v_state = attn_fn(q, k, v)
    p, u_state = pool_branch(u_hist, u, pos0, w_pool, pool_scale)
    mixed = jax.nn.sigmoid(ga) * a + jax.nn.sigmoid(gp) * p
    x = x + rmsnorm(jnp.einsum('btd,de->bte', mixed, w_out), norm_attn_post)
    h2 = rmsnorm(x, norm_mlp_pre)
    f = jnp.einsum('btf,fd->btd', jnp.square(jax.nn.relu(jnp.einsum('btd,df->btf', h2, w_up))), w_down)
    x = x + rmsnorm(f, norm_mlp_post)
    return x, k_state, v_state, u_state


def setup_inputs(seed: int = 0) -> dict:
    key = jax.random.key(seed)
    ks = jax.random.split(key, 16)
    wb = min(WINDOW, PAST_LEN)
    f32 = jnp.float32
    nrm = lambda k, shape, s: jax.random.normal(k, shape, f32) * s
    return {
        "x_prompt": nrm(ks[0], (BATCH, SEQ, D_MODEL), 1.0),
        "x_sample": nrm(ks[1], (DEC_BATCH, DEC_SEQ, D_MODEL), 1.0),
        "cache_k_win": nrm(ks[2], (DEPTH, DEC_BATCH, wb, N_KV_HEADS, HEAD_DIM), 1.0),
        "cache_v_win": nrm(ks[3], (DEPTH, DEC_BATCH, wb, N_KV_HEADS, HEAD_DIM), 1.0),
        "state_pool": nrm(ks[4], (DEPTH, DEC_BATCH, POOL_HIST, D_POOL), 1.0),
        "norm_attn_pre": 1.0 + nrm(ks[5], (DEPTH, D_MODEL), 0.02),
        "norm_attn_post": 1.0 + nrm(ks[6], (DEPTH, D_MODEL), 0.02),
        "w_in": nrm(ks[7], (DEPTH, D_MODEL, D_IN), D_MODEL ** -0.5),
        "attn_sinks": nrm(ks[8], (DEPTH, N_KV_HEADS, GQA_GROUP), 0.5),
        "w_pool": nrm(ks[9], (DEPTH, N_POOL_GROUPS, POOL_GROUP, POOL_OUT_GROUP), POOL_GROUP ** -0.5),
        "pool_scale": 1.0 + nrm(ks[10], (DEPTH, D_MODEL), 0.1),
        "w_out": nrm(ks[11], (DEPTH, D_MODEL, D_MODEL), D_MODEL ** -0.5),
        "norm_mlp_pre": 1.0 + nrm(ks[12], (DEPTH, D_MODEL), 0.02),
        "norm_mlp_post": 1.0 + nrm(ks[13], (DEPTH, D_MODEL), 0.02),
        "w_up": nrm(ks[14], (DEPTH, D_MODEL, D_FF), D_MODEL ** -0.5),
        "w_down": nrm(ks[15], (DEPTH, D_FF, D_MODEL), D_FF ** -0.5),
    }


def reference(x_prompt, x_sample, cache_k_win, cache_v_win, state_pool, norm_attn_pre,
              norm_attn_post, w_in, attn_sinks, w_pool, pool_scale, w_out, norm_mlp_pre,
              norm_mlp_post, w_up, w_down):
    xp, xs = x_prompt, x_sample
    kp_l, vp_l, up_l, ks_l, vs_l, us_l = [], [], [], [], [], []
    for l in range(DEPTH):
        params = (norm_attn_pre[l], norm_attn_post[l], w_in[l], w_pool[l], pool_scale[l],
                  w_out[l], norm_mlp_pre[l], norm_mlp_post[l], w_up[l], w_down[l])
        u0 = jnp.zeros((xp.shape[0], POOL_HIST, D_POOL), xp.dtype)
        attn_p = functools.partial(window_attn_prompt, sinks=attn_sinks[l])
        xp, kp, vp, up = decoder_layer(xp, attn_p, u0, 0, *params)
        attn_s = functools.partial(window_attn_sample, k_hist=cache_k_win[l],
                                   v_hist=cache_v_win[l], sinks=attn_sinks[l])
        xs, kss, vss, uss = decoder_layer(xs, attn_s, state_pool[l], PAST_LEN, *params)
        kp_l.append(kp); vp_l.append(vp); up_l.append(up)
        ks_l.append(kss); vs_l.append(vss); us_l.append(uss)
    y_prompt, y_sample = xp, xs
    return (y_prompt, y_sample, jnp.stack(kp_l), jnp.stack(vp_l), jnp.stack(up_l),
            jnp.stack(ks_l), jnp.stack(vs_l), jnp.stack(us_l))
```

```python
import numpy as np
import concourse.bass as bass
import concourse.mybir as mybir
from concourse.bass_utils import run_bass_kernel_spmd

F32 = mybir.dt.float32
BF16 = mybir.dt.bfloat16
AF = mybir.ActivationFunctionType
ALU = mybir.AluOpType

D = 2048
NCH = 16
SEQ = 2048
DEC_B = 16
DEC_T = 8
HD = 64
NKV = 4
D_IN = 7680
D_FF = 8192
POOL_W = (2, 4, 8, 16)
EPS = 1e-6
N_CORES = 8

SAME_ENGINE_SYNC = True
TILES = ["s", "p0", "p1", "p2", "p3"]
STOP = None
PAD_PE = 0
PAD_DMA = 0
PREFETCH_X = False
NS_SLOTS = 3
USE_SCRATCH = True


class StopBuild(Exception):
    pass


def ckpt(label):
    if STOP == label:
        raise StopBuild(label)


class Buf:
    __slots__ = ("name", "w", "r", "excl")

    def __init__(self, name, excl=False):
        self.name = name
        self.w = None
        self.r = []
        self.excl = excl


class DmaSem:
    _n = 0

    def __init__(self, nc, name):
        self.sem = nc.alloc_semaphore(name)
        self.key = f"dma_{name}_{DmaSem._n}"
        DmaSem._n += 1
        self.val = 0


class Eng:
    def __init__(self, name, h, sem):
        self.name = name
        self.h = h
        self.sem = sem
        self.pos = 0
        self.last = None
        self.last_has_sig = False
        self.nsig = 0
        self.sig_pos = 0
        self.sigs = []
        self.waited = {}

    def wait_sem(self, sem, key, val):
        if self.waited.get(key, 0) >= val:
            return
        self.presignal()
        self.h.wait_ge(sem, val)
        self.waited[key] = val
        self.pos += 1
        self.last = None
        self.last_has_sig = False

    def presignal(self):
        if self.last is not None and not self.last_has_sig and self.sig_pos < self.pos:
            self.signal_for(self.pos)

    def signal_for(self, pos):
        if self.sig_pos >= pos:
            for p, c in self.sigs:
                if p >= pos:
                    return c
            raise AssertionError
        assert self.last is not None and not self.last_has_sig, (self.name, pos, self.pos)
        self.last.then_inc(self.sem, 1)
        self.last_has_sig = True
        self.nsig += 1
        self.sig_pos = self.pos
        self.sigs.append((self.pos, self.nsig))
        return self.nsig


class Trk:
    def __init__(self, nc):
        self.nc = nc
        self.engs = {}
        self.dsems = []

    def add_engine(self, name, h):
        e = Eng(name, h, self.nc.alloc_semaphore(f"s_{name}"))
        self.engs[name] = e
        return e

    def dsem(self, name):
        d = DmaSem(self.nc, name)
        self.dsems.append(d)
        return d

    def _wait(self, e, ev):
        if ev[0] == "c":
            _, pe, pos = ev
            if pe is e:
                if not SAME_ENGINE_SYNC or e.name == "pe":
                    return
                if e.last is None and e.sig_pos < pos:
                    return
            cnt = pe.signal_for(pos)
            e.wait_sem(pe.sem, pe.name, cnt)
        else:
            _, d, val = ev
            e.wait_sem(d.sem, d.key, max(val, d.val))

    def _deps(self, e, reads, writes):
        deps = []
        for b in reads:
            if b.w is not None:
                deps.append(b.w)
            if b.excl:
                deps.extend(ev for ev in b.r if ev[0] == "c" and ev[1] is not e)
        for b in writes:
            if b.w is not None:
                deps.append(b.w)
            deps.extend(b.r)
        best = {}
        for ev in deps:
            if ev[0] == "c":
                k = ("c", ev[1].name)
                if k not in best or best[k][2] < ev[2]:
                    best[k] = ev
            else:
                k = ("d", ev[1].key)
                if k not in best or best[k][2] < ev[2]:
                    best[k] = ev
        for ev in best.values():
            self._wait(e, ev)

    def _record(self, ev, reads, writes):
        for b in reads:
            b.r.append(ev)
        for b in writes:
            b.w = ev
            b.r = []

    def op(self, ename, fn, reads=(), writes=()):
        e = self.engs[ename]
        self._deps(e, reads, writes)
        ins = fn()
        e.pos += 1
        e.last = ins
        e.last_has_sig = False
        self._record(("c", e, e.pos), reads, writes)
        return ins

    def dma(self, qname, out, in_, dsem, reads=(), writes=(), **kw):
        e = self.engs[qname]
        self._deps(e, reads, writes)
        e.presignal()
        ins = e.h.dma_start(out=out, in_=in_, **kw)
        dsem.val += 16
        ins.then_inc(dsem.sem, 16)
        e.pos += 1
        e.last = None
        e.last_has_sig = False
        self._record(("d", dsem, dsem.val), reads, writes)
        return ins

    def barrier(self):
        cnts = {}
        for n, e in self.engs.items():
            if e.pos > 0 and e.sig_pos < e.pos and e.last is not None:
                e.signal_for(e.pos)
            cnts[n] = e.nsig
        for n, e in self.engs.items():
            for n2, e2 in self.engs.items():
                if n2 != n and cnts[n2] > 0:
                    e.wait_sem(e2.sem, e2.name, cnts[n2])
            for d in self.dsems:
                if d.val > 0:
                    e.wait_sem(d.sem, d.key, d.val)


class Rot:
    def __init__(self, items):
        self.items = items
        self.i = 0

    def next(self):
        it = self.items[self.i % len(self.items)]
        self.i += 1
        return it


C_MASK = 0
C_POOL = 1536
C_HIST = C_POOL + 16 * 128
C_BF_END = C_HIST + 4 * 64
C_ID = C_BF_END
C_END = C_ID + 128


def _build_consts():
    c = np.zeros((128, C_END), np.float32)
    j = np.arange(128)[:, None]
    i = np.arange(128)[None, :]
    prev = (j > i).astype(np.float32)
    cur = (j <= i).astype(np.float32)
    M0 = C_MASK
    for b2 in range(2):
        c[:, M0 + (b2 * 2 + 0) * 128: M0 + (b2 * 2 + 1) * 128] = prev
        c[:, M0 + (b2 * 2 + 1) * 128: M0 + (b2 * 2 + 2) * 128] = cur
    M1 = C_MASK + 512
    c[:, M1 + 128: M1 + 256] = cur
    c[:, M1 + 256: M1 + 384] = prev
    c[:, M1 + 384: M1 + 512] = cur
    t_of = np.arange(128) % 8
    b_of = np.arange(128) // 8
    hist = (np.arange(128)[:, None] > t_of[None, :]).astype(np.float32)
    scur = ((b_of[:, None] == b_of[None, :]) & (t_of[:, None] <= t_of[None, :])).astype(np.float32)
    M2 = C_MASK + 1024
    c[:, M2: M2 + 128] = hist
    c[:, M2 + 128: M2 + 256] = scur
    tp = np.arange(128)[:, None]
    t = np.arange(128)[None, :]
    eye = (tp == t).astype(np.float32)
    for g, w in enumerate(POOL_W):
        inw = ((t - tp >= 0) & (t - tp < w)).astype(np.float32)
        c[:, C_POOL + (0 + g) * 128: C_POOL + (1 + g) * 128] = inw / w - eye
        c[:, C_POOL + (4 + g) * 128: C_POOL + (5 + g) * 128] = ((t + 128 - tp) < w).astype(np.float32) / w
        cnt = np.minimum(t + 1, w).astype(np.float32)
        c[:, C_POOL + (8 + g) * 128: C_POOL + (9 + g) * 128] = inw / cnt - eye
        sinw = ((b_of[:, None] == b_of[None, :]) & (t_of[None, :] - t_of[:, None] >= 0)
                & (t_of[None, :] - t_of[:, None] < w)).astype(np.float32)
        c[:, C_POOL + (12 + g) * 128: C_POOL + (13 + g) * 128] = sinw / w - eye
        hb = np.arange(120) // 15
        ht = np.arange(120) % 15
        cb = np.arange(64) // 8
        cs = np.arange(64) % 8
        hm = ((hb[:, None] == cb[None, :]) & (ht[:, None] >= 16 + cs[None, :] - w)).astype(np.float32) / w
        c[:120, C_HIST + g * 64: C_HIST + (g + 1) * 64] = hm
    c[:, C_ID:C_ID + 128] = np.eye(128, dtype=np.float32)
    return c


def build_program():
    nc = bass.Bass("TRN2", target_bir_lowering=False)
    dt = lambda name, shape, kind: nc.dram_tensor(name, shape, F32, kind=kind).ap()
    xp = dt("xp", [SEQ, D], "ExternalInput")
    xs = dt("xs", [128, D], "ExternalInput")
    ck = dt("ck", [DEC_B, 128, 256], "ExternalInput")
    cv = dt("cv", [DEC_B, 128, 256], "ExternalInput")
    spool = dt("spool", [DEC_B, 15, 1024], "ExternalInput")
    w_in = dt("w_in", [D, D_IN], "ExternalInput")
    w_pool = dt("w_pool", [4, 256, 512], "ExternalInput")
    w_out = dt("w_out", [D, D], "ExternalInput")
    w_up = dt("w_up", [D, D_FF], "ExternalInput")
    w_down = dt("w_down", [D_FF, D], "ExternalInput")
    gv_d = dt("gvec", [128, 5 * 16], "ExternalInput")
    sk_d = dt("sinks", [128, 16], "ExternalInput")
    cst_d = dt("cst", [128, C_END], "ExternalInput")
    yp = dt("yp", [SEQ, D], "ExternalOutput")
    ys = dt("ys", [128, D], "ExternalOutput")
    kp = dt("kp", [128, 256], "ExternalOutput")
    vp = dt("vp", [128, 256], "ExternalOutput")
    upo = dt("upo", [15, 1024], "ExternalOutput")
    kso = dt("kso", [DEC_B, 128, 256], "ExternalOutput")
    vso = dt("vso", [DEC_B, 128, 256], "ExternalOutput")
    uso = dt("uso", [DEC_B, 15, 1024], "ExternalOutput")

    w_in_v = w_in.rearrange("(kc p) n -> p kc n", p=128)
    w_out_v = w_out.rearrange("(kc p) n -> p kc n", p=128)
    w_up_v = w_up.rearrange("(kc p) n -> p kc n", p=128)
    w_down_v = w_down.rearrange("(kc p) n -> p kc n", p=128)

    trk = Trk(nc)
    trk.add_engine("pe", nc.tensor)
    trk.add_engine("act", nc.scalar)
    trk.add_engine("dve", nc.vector)
    trk.add_engine("pool", nc.gpsimd)
    trk.add_engine("sp", nc.sync)
    PE, ACT, DVE, POOL, SP = "pe", "act", "dve", "pool", "sp"

    sb = lambda name, shape, dtype: nc.alloc_sbuf_tensor(name, shape, dtype)
    xT = sb("xT", [128, NCH, 512], F32)
    hbuf = sb("hbuf", [128, 8192], BF16)
    mixb = sb("mixb", [128, 8192], BF16)
    fbuf = sb("fbuf", [128, 8192], F32)
    arena = sb("arena", [128, 16384], BF16)
    wsl = sb("wsl", [128, NS_SLOTS, 4096], BF16)
    sg = sb("sg", [128, 4, 512], F32)
    attn = sb("attn", [128, 4, 512], F32)
    sq = sb("sq", [128, 2, 512], BF16)
    rinv = sb("rinv", [128, 2, 512], F32)
    tmpf = sb("tmpf", [128, 2, 512], F32)
    wpool_sb = sb("wpool_sb", [128, 4, 2, 512], BF16)
    cstb = sb("cstb", [128, C_BF_END], BF16)
    identF = sb("identF", [128, 128], F32)
    onesD = sb("onesD", [128, 128], BF16)
    ones64 = sb("ones64", [128, 64], BF16)
    gvec = sb("gvec_sb", [128, 5, 16], F32)
    esink = sb("esink", [128, 16], F32)
    kprev = sb("kprev", [128, 4, 128], BF16)
    pT_sb = sb("pT_sb", [128, 8, 512], BF16)
    vprev = sb("vprev", [128, 256], BF16)
    uprev = sb("uprev", [128, 1024], BF16)

    banks = [nc.alloc_psum_tensor(f"bank{i}", [128, 512], F32) for i in range(8)]
    bank_b = [Buf(f"bank{i}", excl=True) for i in range(8)]
    mm_rot = Rot([0, 1])
    big_rot = Rot([0, 1, 2, 3, 4, 5])
    sc_rot = Rot([(2, 3), (4, 5)])
    AV, DEN, TR = 6, 7, 7

    B_xT = [Buf(f"xT{c}") for c in range(NCH)]
    B_h = [Buf(f"h{c}") for c in range(NCH)]
    B_mix = [Buf(f"mix{c}") for c in range(NCH)]
    B_f = [Buf(f"f{c}") for c in range(NCH)]
    B_up = [Buf(f"up{c}") for c in range(32)]
    B_q = [Buf("q0"), Buf("q1")]
    B_kT2 = Buf("kT2")
    B_v = [Buf(f"v{i}") for i in range(5)]
    B_u = [Buf(f"u{i}") for i in range(5)]
    B_pl = [Buf(f"pl{i}") for i in range(8)]
    B_pT = [Buf(f"pT{i}") for i in range(8)]
    B_ws = [Buf(f"ws{i}") for i in range(4)]
    B_sg = [Buf(f"sg{i}") for i in range(4)]
    B_attn = [Buf(f"attn{i}") for i in range(4)]
    B_sq = [Buf("sq0"), Buf("sq1")]
    B_rinv = [Buf("rinv0"), Buf("rinv1")]
    B_tmp = [Buf("tmp0"), Buf("tmp1")]
    B_cst = Buf("cst")
    B_out = Buf("out_small")
    B_y = Buf("y_out")
    B_khT = Buf("khT")
    B_vh = Buf("vhist")
    B_uh = Buf("uh")
    B_prev = Buf("prev")
    B_ust = [Buf(f"ust{i}") for i in range(4)]
    B_kvst = [Buf("kst"), Buf("vst")]
    SG_ALIAS = {0: [B_ust[0], B_ust[1]], 1: [B_ust[2], B_ust[3]], 2: B_kvst, 3: []}

    wsem = [trk.dsem(f"w{i}") for i in range(4)]
    ssem = [trk.dsem(f"ss{i}") for i in range(4)]
    xsem = [trk.dsem(f"x{i}") for i in range(4)]
    ysem = [trk.dsem(f"y{i}") for i in range(4)]
    csem = trk.dsem("cst")
    csem_p = trk.dsem("cstp")
    hsem_p = trk.dsem("histp")
    osem = trk.dsem("osm")
    osem_kv = [trk.dsem("okv0"), trk.dsem("okv1")]
    osem_u = [trk.dsem(f"ou{i}") for i in range(4)]
    hsem = trk.dsem("hist")

    sq_rot = Rot([0, 1])
    tmp_rot = Rot([0, 1])
    pT_rot = Rot(list(range(8)))
    sga_rot = Rot([0, 1])
    sgp_rot = Rot([2, 3])
    ev_rot = Rot([DVE, ACT])

    trk.dma(SP, identF[:], cst_d[:, C_ID:C_END], csem, writes=[B_cst])
    trk.dma(SP, gvec[:], gv_d.rearrange("p (w c) -> p w c", w=5), csem, writes=[B_cst])
    trk.dma(SP, esink[:], sk_d, csem, writes=[B_cst])
    trk.dma(POOL, cstb[:], cst_d[:, 0:C_BF_END], csem_p, writes=[B_cst])
    trk.dma(POOL, wpool_sb[:], w_pool.rearrange("g (kc p) n -> p g kc n", p=128), csem_p, writes=[B_cst])
    trk.op(DVE, lambda: nc.vector.memset(onesD[:], 1.0 / D), writes=[B_cst])
    trk.op(DVE, lambda: nc.vector.memset(ones64[:], 1.0), writes=[B_cst])
    trk.op(ACT, lambda: nc.scalar.activation(out=esink[:], in_=esink[:], func=AF.Exp), reads=[B_cst], writes=[B_cst])

    G_PRE, G_POST, G_PSC, G_MPRE, G_MPOST = 0, 1, 2, 3, 4

    def gcol(which, c):
        return gvec[:, which, c:c + 1]

    NS = NS_SLOTS

    def tile_blocks():
        bl = []
        for part in range(2):
            bl.append((f"kv{part}", w_in_v[:, :, 2048 + part * 256:2048 + (part + 1) * 256], 16, 256))
        for ub in range(4):
            bl.append((f"u{ub}", w_in_v[:, :, 2560 + ub * 256:2560 + (ub + 1) * 256], 16, 256))
        for g in range(4):
            for nm, base in (("q", 0), ("ga", 3584), ("gp", 5632)):
                for hb in range(2):
                    c0 = base + g * 512 + hb * 256
                    bl.append((f"{nm}{g}_{hb}", w_in_v[:, :, c0:c0 + 256], 16, 256))
        for j in range(8):
            bl.append((f"o{j}", w_out_v[:, :, j * 256:(j + 1) * 256], 16, 256))
        for half in range(2):
            for j in range(16):
                jj = half * 16 + j
                bl.append((f"up{jj}", w_up_v[:, :, jj * 256:(jj + 1) * 256], 16, 256))
            for cp in range(8):
                for kq in range(2):
                    k0 = half * 32 + kq * 16
                    bl.append((f"dn{half}_{cp}_{kq}", w_down_v[:, k0:k0 + 16, cp * 256:(cp + 1) * 256], 16, 256))
        return bl

    blocks1 = tile_blocks()
    NBLK = len(blocks1)
    N_TILES = len(TILES)
    wscr = nc.dram_tensor("wscr", [NBLK, 128, 4096], BF16, kind="Internal").ap()
    B_scr = [Buf(f"scr{j}") for j in range(NBLK)]
    wstate = {"issued": 0, "cons": 0}
    total_blocks = NBLK * N_TILES

    def w_issue():
        i = wstate["issued"]
        if i >= total_blocks:
            return
        j = i % NBLK
        name, src, kcn, ncol = blocks1[j]
        s_ = i % NS
        dst = wsl[:, s_, 0:kcn * ncol].rearrange("p (k n) -> p k n", k=kcn)
        tix = i // NBLK
        J0 = NBLK // 2 if N_TILES > 2 else NBLK
        convert_here = (tix == 0 and j < J0) or (tix == 1 and j >= J0)
        from_fp32 = (tix == 0) or (tix == 1 and j >= J0) or not USE_SCRATCH
        if from_fp32:
            trk.dma(POOL, dst, src, wsem[s_], writes=[B_ws[s_]])
            if N_TILES > 1 and USE_SCRATCH and convert_here:
                trk.dma(SP, wscr[j], wsl[:, s_, :], ssem[s_], reads=[B_ws[s_]], writes=[B_scr[j]])
        else:
            trk.dma(POOL, dst, wscr[j].rearrange("p (k n) -> p k n", k=kcn), wsem[s_], reads=[B_scr[j]],
                    writes=[B_ws[s_]])
        wstate["issued"] += 1

    def w_next(expect):
        i = wstate["cons"]
        name, src, kcn, ncol = blocks1[i % NBLK]
        assert name == expect, (name, expect)
        while wstate["issued"] <= i:
            w_issue()
        s_ = i % NS
        wstate["cons"] += 1
        view = wsl[:, s_, 0:kcn * ncol].rearrange("p (k n) -> p k n", k=kcn)
        return view, B_ws[s_]

    def w_done():
        while wstate["issued"] < min(total_blocks, wstate["cons"] + NS):
            w_issue()

    for _ in range(NS):
        w_issue()

    def stats_rinv(ri):
        r = rinv[:, ri, :]
        trk.op(DVE, lambda: nc.vector.tensor_scalar(out=r[:, 0:TT[0]], in0=banks[TR][:, 0:TT[0]], scalar1=EPS,
                                                    scalar2=None, op0=ALU.add),
               reads=[bank_b[TR]], writes=[B_rinv[ri]])
        trk.op(ACT, lambda: nc.scalar.activation(out=r[:, 0:TT[0]], in_=r[:, 0:TT[0]], func=AF.Sqrt),
               reads=[B_rinv[ri]], writes=[B_rinv[ri]])
        trk.op(DVE, lambda: nc.vector.reciprocal(out=r[:, 0:TT[0]], in_=r[:, 0:TT[0]]),
               reads=[B_rinv[ri]], writes=[B_rinv[ri]])

    TT = [512]
    pf_state = {}
    NEXT_PROMPT = {0: 1, 2: 3}

    def stat_mm(si, c):
        T = TT[0]
        trk.op(PE, lambda: nc.tensor.matmul(banks[TR][:, 0:T], lhsT=onesD[:], rhs=sq[:, si, 0:T],
                                            start=(c == 0), stop=(c == NCH - 1)),
               reads=[B_sq[si], B_cst], writes=[bank_b[TR]])

    def evac_copy(eng, out, in_, reads, writes, scale=None):
        if eng == DVE:
            if scale is None:
                trk.op(DVE, lambda: nc.vector.tensor_copy(out=out, in_=in_), reads=reads, writes=writes)
            else:
                trk.op(DVE, lambda: nc.vector.tensor_scalar(out=out, in0=in_, scalar1=scale, scalar2=None,
                                                            op0=ALU.mult), reads=reads, writes=writes)
        else:
            if scale is None:
                trk.op(ACT, lambda: nc.scalar.activation(out=out, in_=in_, func=AF.Copy), reads=reads, writes=writes)
            else:
                trk.op(ACT, lambda: nc.scalar.activation(out=out, in_=in_, func=AF.Copy, scale=scale),
                       reads=reads, writes=writes)

    def run_tile(kind, ti):
        sample = (kind == "s")
        T = 128 if sample else 512
        TT[0] = T
        nb = T // 128
        first = (not sample) and ti == 0
        last = (not sample) and ti == 3
        hT = hbuf[:, 0:NCH * T].rearrange("p (c t) -> p c t", c=NCH)
        mixT = mixb[:, 0:NCH * T].rearrange("p (c t) -> p c t", c=NCH)
        fT = fbuf[:, 0:NCH * T].rearrange("p (c t) -> p c t", c=NCH)
        a0 = 0
        upT = arena[:, 0:32 * T].rearrange("p (c t) -> p c t", c=32)
        qT = arena[:, a0:a0 + 2 * T].rearrange("p (c t) -> p c t", c=2); a0 += 2 * T
        KW = T + 128
        kT2 = arena[:, a0:a0 + 4 * KW].rearrange("p (h t) -> p h t", h=4); a0 += 4 * KW
        vbuf = arena[:, a0:a0 + (nb + 1) * 256].rearrange("p (b c) -> p b c", c=256); a0 += (nb + 1) * 256
        ubuf = arena[:, a0:a0 + (nb + 1) * 1024].rearrange("p (b c) -> p b c", c=1024); a0 += (nb + 1) * 1024
        pooledT = arena[:, a0:a0 + 8 * T].rearrange("p (c t) -> p c t", c=8); a0 += 8 * T
        pT = pT_sb
        assert a0 <= 16384 and (not sample or a0 <= 8192)
        if sample:
            khT = arena[:, 8192:16384].rearrange("p (b h k) -> p b h k", b=16, h=4)
            vhist = mixb[:, 2048:2048 + 4096].rearrange("p (b c) -> p b c", b=16)
            uh = mixb[:, 6144:8192].rearrange("p (g c) -> p g c", g=2)

        ckpt("pre")
        prefetched = (not sample) and PREFETCH_X and pf_state.get("tile") == ti
        mixF = mixb[:].bitcast(F32)
        hF = hbuf[:].bitcast(F32)

        def x_stage(tb, c):
            if not prefetched:
                return fbuf[:, tb * 2048 + c * 128: tb * 2048 + (c + 1) * 128], B_f[4 * tb + c // 4]
            src_t, bl = (mixF, B_mix) if tb < 2 else (hF, B_h)
            o = (tb % 2) * 2048 + c * 128
            return src_t[:, o:o + 128], bl[8 * (tb % 2) + c // 2]

        if not prefetched:
            for tb in range(nb):
                src = xs if sample else xp[ti * 512 + tb * 128: ti * 512 + (tb + 1) * 128, :]
                trk.dma(SP, fbuf[:, tb * 2048:(tb + 1) * 2048], src, xsem[tb], writes=B_f[4 * tb:4 * tb + 4])
        if sample:
            trk.dma(POOL, vhist, cv.rearrange("b k c -> k b c"), hsem_p, writes=[B_vh])
            trk.dma(POOL, uh[0:120], spool.rearrange("(g b) t c -> (b t) g c", g=2), hsem_p, writes=[B_uh])
            trk.dma(SP, kso[:, 0:120, :], ck[:, 8:128, :], osem, writes=[B_out])
            trk.dma(SP, vso[:, 0:120, :], cv[:, 8:128, :], osem, writes=[B_out])
            trk.dma(SP, uso[:, 0:7, :], spool[:, 8:15, :], osem, writes=[B_out])
        ckpt("loads")
        if first:
            trk.op(DVE, lambda: nc.vector.memset(kT2[:, :, 0:128], 0.0), writes=[B_kT2])
            trk.op(DVE, lambda: nc.vector.memset(vbuf[:, 0, :], 0.0), writes=[B_v[0]])
            trk.op(DVE, lambda: nc.vector.memset(ubuf[:, 0, :], 0.0), writes=[B_u[0]])
        elif not sample:
            trk.op(DVE, lambda: nc.vector.tensor_copy(out=kT2[:, :, 0:128], in_=kprev[:]), reads=[B_prev],
                   writes=[B_kT2] + B_up)
            trk.op(DVE, lambda: nc.vector.tensor_copy(out=vbuf[:, 0, :], in_=vprev[:]), reads=[B_prev], writes=[B_v[0]])
            trk.op(DVE, lambda: nc.vector.tensor_copy(out=ubuf[:, 0, :], in_=uprev[:]), reads=[B_prev], writes=[B_u[0]])

        pend = []
        for c in range(NCH):
            bk = big_rot.next()
            for tb in range(nb):
                xs_ap, xs_b = x_stage(tb, c)
                trk.op(PE, lambda tb=tb: nc.tensor.transpose(banks[bk][:, tb * 128:(tb + 1) * 128], xs_ap, identF[:]),
                       reads=[xs_b, B_cst], writes=[bank_b[bk]])
            si = sq_rot.next()
            trk.op(DVE, lambda: nc.vector.tensor_copy(out=xT[:, c, 0:T], in_=banks[bk][:, 0:T]),
                   reads=[bank_b[bk]], writes=[B_xT[c]])
            trk.op(ACT, lambda: nc.scalar.activation(out=sq[:, si, 0:T], in_=xT[:, c, 0:T], func=AF.Square),
                   reads=[B_xT[c]], writes=[B_sq[si]])
            pend.append((si, c))
            if len(pend) > 1:
                stat_mm(*pend.pop(0))
        while pend:
            stat_mm(*pend.pop(0))
        ckpt("p1t")
        stats_rinv(0)
        ckpt("p1r")
        for c in range(NCH):
            trk.op(DVE, lambda: nc.vector.scalar_tensor_tensor(out=hT[:, c, :], in0=xT[:, c, 0:T], scalar=gcol(G_PRE, c),
                                                               in1=rinv[:, 0, 0:T], op0=ALU.mult, op1=ALU.mult),
                   reads=[B_xT[c], B_rinv[0], B_cst], writes=[B_h[c]])

        ckpt("p1")
        if sample:
            stg8 = attn[:].rearrange("p a (b c) -> p (a b) c", b=2)
            for r in range(2):
                trk.dma(SP, stg8, ck[8 * r:8 * r + 8].rearrange("b k c -> k b c"), hsem, writes=B_attn)
                for b8 in range(8):
                    ti_ = tmp_rot.next()
                    kd = tmpf[:, ti_, :].rearrange("p (h e d) -> p h e d", h=4, e=2)
                    ksrc = stg8[:, b8, :].rearrange("p (h d) -> p h d", h=4)
                    for e in range(2):
                        trk.op(DVE, lambda e=e: nc.vector.tensor_copy(out=kd[:, :, e, :], in_=ksrc),
                               reads=[B_attn[b8 // 2]], writes=[B_tmp[ti_]])
                    bk = big_rot.next()
                    for h in range(4):
                        trk.op(PE, lambda h=h: nc.tensor.transpose(banks[bk][:, h * 128:(h + 1) * 128],
                                                                   tmpf[:, ti_, h * 128:(h + 1) * 128], identF[:]),
                               reads=[B_tmp[ti_], B_cst], writes=[bank_b[bk]])
                    evac_copy(ACT, khT[:, 8 * r + b8, :, :],
                              banks[bk][:].rearrange("p (h k) -> p h k", h=4), [bank_b[bk]], [B_khT])

        ckpt("hist")
        for part in range(2):
            wv, wb = w_next(f"kv{part}")
            for tb in range(nb):
                bk = mm_rot.next()
                for kc in range(NCH):
                    trk.op(PE, lambda kc=kc: nc.tensor.matmul(banks[bk][:, 0:256], lhsT=hT[:, kc, tb * 128:(tb + 1) * 128],
                                                              rhs=wv[:, kc, :], start=(kc == 0), stop=(kc == NCH - 1)),
                           reads=[B_h[kc], wb], writes=[bank_b[bk]])
                stg_out = sample or (last and tb == nb - 1)
                if stg_out:
                    trk.op(DVE, lambda: nc.vector.tensor_copy(out=sg[:, 2, part * 256:(part + 1) * 256],
                                                              in_=banks[bk][:, 0:256]),
                           reads=[bank_b[bk], B_sg[2]], writes=[B_kvst[part]])
                    if sample:
                        dst_o = kso if part == 0 else vso
                        for bt in range(DEC_B):
                            trk.dma(SP, dst_o[bt, 120:128, :], sg[bt * 8:(bt + 1) * 8, 2, part * 256:(part + 1) * 256],
                                    osem_kv[part], reads=[B_kvst[part]], writes=[B_out])
                    else:
                        trk.dma(SP, kp if part == 0 else vp, sg[:, 2, part * 256:(part + 1) * 256], osem_kv[part],
                                reads=[B_kvst[part]], writes=[B_out])
                if part == 1:
                    evac_copy(DVE if stg_out else ev_rot.next(), vbuf[:, 1 + tb, :], banks[bk][:, 0:256], [bank_b[bk]],
                              [B_v[1 + tb]])
                    continue
                kd = attn[:, tb % 2, :].rearrange("p (h e d) -> p h e d", h=4, e=2)
                ksrc = banks[bk][:, 0:256].rearrange("p (h d) -> p h d", h=4)
                trk.op(DVE, lambda: nc.vector.tensor_copy(out=kd[:, :, 0, :], in_=ksrc), reads=[bank_b[bk]],
                       writes=[B_attn[tb % 2]])
                trk.op(DVE, lambda: nc.vector.tensor_copy(out=kd[:, :, 1, :], in_=ksrc), reads=[bank_b[bk]],
                       writes=[B_attn[tb % 2]])
                for h in range(4):
                    trk.op(PE, lambda h=h: nc.tensor.transpose(banks[TR][:, h * 128:(h + 1) * 128],
                                                               attn[:, tb % 2, h * 128:(h + 1) * 128], identF[:]),
                           reads=[B_attn[tb % 2], B_cst], writes=[bank_b[TR]])
                trk.op(DVE, lambda: nc.vector.tensor_copy(out=kT2[:, :, 128 + tb * 128:128 + (tb + 1) * 128],
                                                          in_=banks[TR][:].rearrange("p (h k) -> p h k", h=4)),
                       reads=[bank_b[TR]], writes=[B_kT2])
            w_done()

        ckpt("p2a")
        sgf = sg[:].rearrange("p a n -> p (a n)")
        for ub in range(4):
            wv, wb = w_next(f"u{ub}")
            for tb in range(nb):
                bk = mm_rot.next()
                for kc in range(NCH):
                    trk.op(PE, lambda kc=kc: nc.tensor.matmul(banks[bk][:, 0:256], lhsT=hT[:, kc, tb * 128:(tb + 1) * 128],
                                                              rhs=wv[:, kc, :], start=(kc == 0), stop=(kc == NCH - 1)),
                           reads=[B_h[kc], wb], writes=[bank_b[bk]])
                stg_out = sample or (last and tb == nb - 1)
                evac_copy(DVE if stg_out else ev_rot.next(), ubuf[:, 1 + tb, ub * 256:(ub + 1) * 256],
                          banks[bk][:, 0:256], [bank_b[bk]], [B_u[1 + tb]])
                if stg_out:
                    trk.op(DVE, lambda: nc.vector.tensor_copy(out=sgf[:, ub * 256:(ub + 1) * 256], in_=banks[bk][:, 0:256]),
                           reads=[bank_b[bk], B_sg[ub // 2]], writes=[B_ust[ub]])
                    if sample:
                        for bt in range(DEC_B):
                            trk.dma(SP, uso[bt, 7:15, ub * 256:(ub + 1) * 256],
                                    sgf[bt * 8:(bt + 1) * 8, ub * 256:(ub + 1) * 256],
                                    osem_u[ub], reads=[B_ust[ub]], writes=[B_out])
                    else:
                        trk.dma(SP, upo[:, ub * 256:(ub + 1) * 256], sgf[113:128, ub * 256:(ub + 1) * 256], osem_u[ub],
                                reads=[B_ust[ub]], writes=[B_out])
            w_done()
        cp = lambda idx: cstb[:, C_POOL + idx * 128: C_POOL + (idx + 1) * 128]
        for uc in range(8):
            g = uc // 2
            bk = mm_rot.next()
            if sample:
                trk.op(PE, lambda: nc.tensor.matmul(banks[bk][:, 0:128], lhsT=ubuf[:, 1, uc * 128:(uc + 1) * 128],
                                                    rhs=cp(12 + g), start=True, stop=False),
                       reads=[B_u[1], B_cst], writes=[bank_b[bk]])
                for gr in range(2):
                    trk.op(PE, lambda gr=gr: nc.tensor.matmul(banks[bk][:, gr * 64:(gr + 1) * 64],
                                                              lhsT=uh[0:120, gr, uc * 128:(uc + 1) * 128],
                                                              rhs=cstb[0:120, C_HIST + g * 64:C_HIST + (g + 1) * 64],
                                                              start=False, stop=(gr == 1)),
                           reads=[B_uh, B_cst], writes=[bank_b[bk]])
            else:
                for tb in range(nb):
                    fb = first and tb == 0
                    if not fb:
                        trk.op(PE, lambda tb=tb: nc.tensor.matmul(banks[bk][:, tb * 128:(tb + 1) * 128],
                                                                  lhsT=ubuf[:, tb, uc * 128:(uc + 1) * 128],
                                                                  rhs=cp(4 + g), start=True, stop=False),
                               reads=[B_u[tb], B_cst], writes=[bank_b[bk]])
                    trk.op(PE, lambda tb=tb, fb=fb: nc.tensor.matmul(banks[bk][:, tb * 128:(tb + 1) * 128],
                                                                     lhsT=ubuf[:, 1 + tb, uc * 128:(uc + 1) * 128],
                                                                     rhs=cp(8 + g) if fb else cp(g),
                                                                     start=fb, stop=True),
                           reads=[B_u[1 + tb], B_cst], writes=[bank_b[bk]])
            evac_copy(ev_rot.next(), pooledT[:, uc, :], banks[bk][:, 0:T], [bank_b[bk]], [B_pl[uc]])

        ckpt("p2b")
        for g in range(4):
            qw = {}

            def q_gemm(ci, g=g):
                c = 4 * g + ci
                if ci % 2 == 0:
                    qw["w"] = w_next(f"q{g}_{ci // 2}")
                wv, wb = qw["w"]
                cl = ci % 2
                bk = mm_rot.next()
                for kc in range(NCH):
                    trk.op(PE, lambda kc=kc: nc.tensor.matmul(banks[bk][:, 0:T], lhsT=wv[:, kc, cl * 128:(cl + 1) * 128],
                                                              rhs=hT[:, kc, :], start=(kc == 0), stop=(kc == NCH - 1)),
                           reads=[B_h[kc], wb], writes=[bank_b[bk]])
                qi = c % 2
                evac_copy(ACT, qT[:, qi, :], banks[bk][:, 0:T], [bank_b[bk]], [B_q[qi]], scale=0.125)
                if ci % 2 == 1:
                    w_done()

            def scores(ci, bp, g=g):
                c = 4 * g + ci
                qi = c % 2
                scp = sc_rot.next()
                pis = []
                for e in range(2):
                    sc = scp[e]
                    pr = slice(64 * e, 64 * e + 64)
                    if sample:
                        for bt in range(DEC_B):
                            trk.op(PE, lambda bt=bt: nc.tensor.matmul(
                                banks[sc][:, bt * 8:bt * 8 + 8],
                                lhsT=khT[pr, bt, g, :], rhs=qT[pr, qi, bt * 8:bt * 8 + 8], start=True, stop=True),
                                reads=[B_khT, B_q[qi]], writes=[bank_b[sc]])
                        trk.op(PE, lambda: nc.tensor.matmul(banks[sc][:, 128:256],
                                                            lhsT=kT2[pr, g, 128:256], rhs=qT[pr, qi, 0:128],
                                                            start=True, stop=True),
                               reads=[B_kT2, B_q[qi]], writes=[bank_b[sc]])
                    else:
                        for b2 in range(2):
                            b = 2 * bp + b2
                            for kb in range(2):
                                trk.op(PE, lambda kb=kb, b=b, b2=b2: nc.tensor.matmul(
                                    banks[sc][:, (b2 * 2 + kb) * 128:(b2 * 2 + kb + 1) * 128],
                                    lhsT=kT2[pr, g, (b + kb) * 128:(b + kb + 1) * 128],
                                    rhs=qT[pr, qi, b * 128:(b + 1) * 128], start=True, stop=True),
                                    reads=[B_kT2, B_q[qi]], writes=[bank_b[sc]])
                    W = 256 if sample else 512
                    pi = pT_rot.next()
                    trk.op(ACT, lambda: nc.scalar.activation(out=pT[:, pi, 0:W], in_=banks[sc][:, 0:W], func=AF.Exp),
                           reads=[bank_b[sc]], writes=[B_pT[pi]])
                    mi = 2 if sample else (1 if (first and bp == 0) else 0)
                    trk.op(DVE, lambda: nc.vector.tensor_tensor(out=pT[:, pi, 0:W], in0=pT[:, pi, 0:W],
                                                                in1=cstb[:, C_MASK + mi * 512:C_MASK + mi * 512 + W],
                                                                op=ALU.mult),
                           reads=[B_pT[pi], B_cst], writes=[B_pT[pi]])
                    pis.append(pi)
                return pis

            def pv(ci, b, pis, g=g):
                b2 = b % 2
                for (obank, is_den) in ((AV, False), (DEN, True)):
                    for e in range(2):
                        pi = pis[e]
                        po = slice(64 * e, 64 * e + 64)
                        tp_ = (0, 64) if e == 1 else None
                        if sample:
                            lh = ones64[:, :] if is_den else vbuf[:, 1, g * 64:(g + 1) * 64]
                            trk.op(PE, lambda lh=lh: nc.tensor.matmul(
                                banks[obank][po, 0:128], lhsT=lh, rhs=pT[:, pi, 128:256],
                                start=True, stop=False, tile_position=tp_),
                                reads=[B_pT[pi], B_v[1], B_cst], writes=[bank_b[obank]])
                            for bt in range(DEC_B):
                                lh = ones64[:, :] if is_den else vhist[:, bt, g * 64:(g + 1) * 64]
                                trk.op(PE, lambda lh=lh, bt=bt: nc.tensor.matmul(
                                    banks[obank][po, bt * 8:bt * 8 + 8], lhsT=lh,
                                    rhs=pT[:, pi, bt * 8:bt * 8 + 8],
                                    start=False, stop=(bt == DEC_B - 1), tile_position=tp_),
                                    reads=[B_pT[pi], B_vh, B_cst], writes=[bank_b[obank]])
                        else:
                            for kb in range(2):
                                lh = ones64[:, :] if is_den else vbuf[:, b + kb, g * 64:(g + 1) * 64]
                                trk.op(PE, lambda lh=lh, kb=kb: nc.tensor.matmul(
                                    banks[obank][po, b * 128:(b + 1) * 128], lhsT=lh,
                                    rhs=pT[:, pi, (b2 * 2 + kb) * 128:(b2 * 2 + kb + 1) * 128],
                                    start=(kb == 0), stop=(kb == 1), tile_position=tp_),
                                    reads=[B_pT[pi], B_v[b + kb], B_cst], writes=[bank_b[obank]])

            def att_scores(ci):
                npair = 1 if sample else nb // 2
                return [scores(ci, bp) for bp in range(npair)]

            def att_pv(ci, pp, g=g):
                c = 4 * g + ci
                if sample:
                    pv(ci, 0, pp[0])
                else:
                    for b in range(nb):
                        pv(ci, b, pp[b // 2])
                ti_ = tmp_rot.next()
                trk.op(DVE, lambda: nc.vector.tensor_scalar(out=tmpf[:, ti_, 0:T], in0=banks[DEN][:, 0:T],
                                                            scalar1=esink[:, c:c + 1], scalar2=None, op0=ALU.add),
                       reads=[bank_b[DEN], B_cst], writes=[B_tmp[ti_]])
                trk.op(DVE, lambda: nc.vector.reciprocal(out=tmpf[:, ti_, 0:T], in_=tmpf[:, ti_, 0:T]),
                       reads=[B_tmp[ti_]], writes=[B_tmp[ti_]])
                trk.op(DVE, lambda: nc.vector.tensor_tensor(out=attn[:, ci, 0:T], in0=banks[AV][:, 0:T],
                                                            in1=tmpf[:, ti_, 0:T], op=ALU.mult),
                       reads=[bank_b[AV], B_tmp[ti_]], writes=[B_attn[ci]])

            gaw = {}

            def ga_chunk(ci, g=g):
                if ci % 2 == 0:
                    gaw["w"] = w_next(f"ga{g}_{ci // 2}")
                wv, wb = gaw["w"]
                cl = ci % 2
                bk = mm_rot.next()
                for kc in range(NCH):
                    trk.op(PE, lambda kc=kc: nc.tensor.matmul(banks[bk][:, 0:T], lhsT=wv[:, kc, cl * 128:(cl + 1) * 128],
                                                              rhs=hT[:, kc, :], start=(kc == 0), stop=(kc == NCH - 1)),
                           reads=[B_h[kc], wb], writes=[bank_b[bk]])
                si_ = sga_rot.next()
                trk.op(ACT, lambda: nc.scalar.activation(out=sg[:, si_, 0:T], in_=banks[bk][:, 0:T], func=AF.Sigmoid),
                       reads=[bank_b[bk]], writes=[B_sg[si_]] + SG_ALIAS[si_])
                trk.op(DVE, lambda: nc.vector.tensor_tensor(out=attn[:, ci, 0:T], in0=attn[:, ci, 0:T],
                                                            in1=sg[:, si_, 0:T], op=ALU.mult),
                       reads=[B_attn[ci], B_sg[si_]], writes=[B_attn[ci]])
                if ci % 2 == 1:
                    w_done()

            q_gemm(0)
            q_gemm(1)
            ckpt("qg")
            p0_ = att_scores(0)
            q_gemm(2)
            att_pv(0, p0_)
            ckpt("att0")
            p1_ = att_scores(1)
            q_gemm(3)
            att_pv(1, p1_)
            p2_ = att_scores(2)
            ga_chunk(0)
            att_pv(2, p2_)
            p3_ = att_scores(3)
            ga_chunk(1)
            att_pv(3, p3_)
            ckpt("g_q")
            ga_chunk(2)
            ga_chunk(3)
            ckpt("g_ga")
            for ci in range(4):
                c = 4 * g + ci
                if ci % 2 == 0:
                    wv, wb = w_next(f"gp{g}_{ci // 2}")
                cl = ci % 2
                bk = mm_rot.next()
                for kc in range(NCH):
                    trk.op(PE, lambda kc=kc: nc.tensor.matmul(banks[bk][:, 0:T], lhsT=wv[:, kc, cl * 128:(cl + 1) * 128],
                                                              rhs=hT[:, kc, :], start=(kc == 0), stop=(kc == NCH - 1)),
                           reads=[B_h[kc], wb], writes=[bank_b[bk]])
                si_ = sgp_rot.next()
                trk.op(ACT, lambda: nc.scalar.activation(out=sg[:, si_, 0:T], in_=banks[bk][:, 0:T], func=AF.Sigmoid),
                       reads=[bank_b[bk]], writes=[B_sg[si_]] + SG_ALIAS[si_])
                bk2 = mm_rot.next()
                for k2 in range(2):
                    trk.op(PE, lambda k2=k2: nc.tensor.matmul(banks[bk2][:, 0:T],
                                                              lhsT=wpool_sb[:, g, k2, ci * 128:(ci + 1) * 128],
                                                              rhs=pooledT[:, 2 * g + k2, :], start=(k2 == 0), stop=(k2 == 1)),
                           reads=[B_pl[2 * g + k2], B_cst], writes=[bank_b[bk2]])
                ti_ = tmp_rot.next()
                trk.op(DVE, lambda: nc.vector.scalar_tensor_tensor(out=tmpf[:, ti_, 0:T], in0=banks[bk2][:, 0:T],
                                                                   scalar=gcol(G_PSC, c), in1=sg[:, si_, 0:T],
                                                                   op0=ALU.mult, op1=ALU.mult),
                       reads=[bank_b[bk2], B_sg[si_], B_cst], writes=[B_tmp[ti_]])
                trk.op(DVE, lambda: nc.vector.tensor_tensor(out=mixT[:, c, :], in0=attn[:, ci, 0:T],
                                                            in1=tmpf[:, ti_, 0:T], op=ALU.add),
                       reads=[B_attn[ci], B_tmp[ti_]], writes=[B_mix[c]])
                if ci % 2 == 1:
                    w_done()

        ckpt("p2c")
        if not sample and not last:
            trk.op(DVE, lambda: nc.vector.tensor_copy(out=kprev[:], in_=kT2[:, :, 512:640]),
                   reads=[B_kT2], writes=[B_prev])
            trk.op(DVE, lambda: nc.vector.tensor_copy(out=vprev[:], in_=vbuf[:, 4, :]), reads=[B_v[4]],
                   writes=[B_prev])
            trk.op(DVE, lambda: nc.vector.tensor_copy(out=uprev[:], in_=ubuf[:, 4, :]), reads=[B_u[4]],
                   writes=[B_prev])

        pend = []
        for j in range(8):
            wv, wb = w_next(f"o{j}")
            for ci in range(2):
                c = 2 * j + ci
                bk = big_rot.next()
                for kc in range(NCH):
                    trk.op(PE, lambda kc=kc: nc.tensor.matmul(banks[bk][:, 0:T], lhsT=wv[:, kc, ci * 128:(ci + 1) * 128],
                                                              rhs=mixT[:, kc, :], start=(kc == 0), stop=(kc == NCH - 1)),
                           reads=[B_mix[kc], wb], writes=[bank_b[bk]])
                si = sq_rot.next()
                trk.op(DVE, lambda: nc.vector.tensor_copy(out=fT[:, c, :], in_=banks[bk][:, 0:T]),
                       reads=[bank_b[bk]], writes=[B_f[c]])
                trk.op(ACT, lambda: nc.scalar.activation(out=sq[:, si, 0:T], in_=fT[:, c, :], func=AF.Square),
                       reads=[B_f[c]], writes=[B_sq[si]])
                pend.append((si, c))
                if len(pend) > 1:
                    stat_mm(*pend.pop(0))
            w_done()
        nxt = NEXT_PROMPT.get(ti) if (not sample and PREFETCH_X) else None
        if nxt is not None:
            for tb in range(2):
                trk.dma(SP, mixF[:, tb * 2048:(tb + 1) * 2048], xp[nxt * 512 + tb * 128: nxt * 512 + (tb + 1) * 128, :],
                        xsem[tb], writes=B_mix[8 * tb:8 * tb + 8])
        while pend:
            stat_mm(*pend.pop(0))
        stats_rinv(1)
        pend = []
        for c in range(NCH):
            ti_ = tmp_rot.next()
            trk.op(DVE, lambda: nc.vector.scalar_tensor_tensor(out=tmpf[:, ti_, 0:T], in0=fT[:, c, :], scalar=gcol(G_POST, c),
                                                               in1=rinv[:, 1, 0:T], op0=ALU.mult, op1=ALU.mult),
                   reads=[B_f[c], B_rinv[1], B_cst], writes=[B_tmp[ti_]])
            trk.op(DVE, lambda: nc.vector.tensor_tensor(out=xT[:, c, 0:T], in0=xT[:, c, 0:T], in1=tmpf[:, ti_, 0:T],
                                                        op=ALU.add),
                   reads=[B_xT[c], B_tmp[ti_]], writes=[B_xT[c]])
            trk.op(ACT, lambda: nc.scalar.activation(out=hT[:, c, :], in_=xT[:, c, 0:T], func=AF.Copy,
                                                     scale=gcol(G_MPRE, c)),
                   reads=[B_xT[c], B_cst], writes=[B_h[c]])
        r3sq = attn[:, 0, 0:T]
        r3q = attn[:, 1, 0:T]

        def x1_stat(c):
            pi = pT_rot.next()
            trk.op(ACT, lambda: nc.scalar.activation(out=pT_sb[:, pi, 0:T], in_=xT[:, c, 0:T], func=AF.Square),
                   reads=[B_xT[c]], writes=[B_pT[pi]])
            return (pi, c)

        def x1_stat_mm(pi, c):
            trk.op(PE, lambda: nc.tensor.matmul(banks[TR][:, 0:T], lhsT=onesD[:], rhs=pT_sb[:, pi, 0:T],
                                                start=(c == 0), stop=(c == NCH - 1)),
                   reads=[B_pT[pi], B_cst], writes=[bank_b[TR]])

        def x1_rinv():
            trk.op(DVE, lambda: nc.vector.tensor_scalar(out=r3sq, in0=banks[TR][:, 0:T], scalar1=EPS, scalar2=None,
                                                        op0=ALU.add), reads=[bank_b[TR]], writes=[B_attn[0]])
            trk.op(DVE, lambda: nc.vector.reciprocal(out=r3sq, in_=r3sq), reads=[B_attn[0]], writes=[B_attn[0]])
            trk.op(DVE, lambda: nc.vector.tensor_tensor(out=r3q, in0=r3sq, in1=r3sq, op=ALU.mult), reads=[B_attn[0]],
                   writes=[B_attn[1]])

        ckpt("p3")
        pend = []
        for half in range(2):
            for j in range(16):
                if half == 0:
                    pend.append(x1_stat(j))
                    if len(pend) > 4:
                        x1_stat_mm(*pend.pop(0))
                wv, wb = w_next(f"up{half * 16 + j}")
                for ci in range(2):
                    uc = 2 * j + ci
                    bk = big_rot.next()
                    for kc in range(NCH):
                        trk.op(PE, lambda kc=kc: nc.tensor.matmul(banks[bk][:, 0:T],
                                                                  lhsT=wv[:, kc, ci * 128:(ci + 1) * 128],
                                                                  rhs=hT[:, kc, :], start=(kc == 0), stop=(kc == NCH - 1)),
                               reads=[B_h[kc], wb], writes=[bank_b[bk]])
                    ti_ = tmp_rot.next()
                    trk.op(ACT, lambda: nc.scalar.activation(out=tmpf[:, ti_, 0:T], in_=banks[bk][:, 0:T], func=AF.Relu),
                           reads=[bank_b[bk]], writes=[B_tmp[ti_]])
                    trk.op(DVE, lambda: nc.vector.tensor_tensor(out=upT[:, uc, :], in0=tmpf[:, ti_, 0:T],
                                                                in1=tmpf[:, ti_, 0:T], op=ALU.mult),
                           reads=[B_tmp[ti_]], writes=[B_up[uc]])
                w_done()
            if half == 0:
                while pend:
                    x1_stat_mm(*pend.pop(0))
                x1_rinv()
            if half == 1 and nxt is not None:
                for tb in range(2, 4):
                    trk.dma(SP, hF[:, (tb - 2) * 2048:(tb - 1) * 2048],
                            xp[nxt * 512 + tb * 128: nxt * 512 + (tb + 1) * 128, :], xsem[tb],
                            writes=B_h[8 * (tb - 2):8 * (tb - 2) + 8])
                pf_state["tile"] = nxt
            for cp in range(8):
                bks = (big_rot.next(), big_rot.next())
                for kq in range(2):
                    wv, wb = w_next(f"dn{half}_{cp}_{kq}")
                    for ci in range(2):
                        bk = bks[ci]
                        for kc in range(16):
                            kk = kq * 16 + kc
                            trk.op(PE, lambda kc=kc, kk=kk: nc.tensor.matmul(
                                banks[bk][:, 0:T], lhsT=wv[:, kc, ci * 128:(ci + 1) * 128], rhs=upT[:, kk, :],
                                start=(kq == 0 and kc == 0), stop=(kq == 1 and kc == 15)),
                                reads=[B_up[kk], wb], writes=[bank_b[bk]])
                    w_done()
                for ci in range(2):
                    c = 2 * cp + ci
                    bk = bks[ci]
                    if half == 0:
                        trk.op(DVE, lambda: nc.vector.tensor_copy(out=fT[:, c, :], in_=banks[bk][:, 0:T]),
                               reads=[bank_b[bk]], writes=[B_f[c]])
                    else:
                        trk.op(DVE, lambda: nc.vector.tensor_tensor(out=fT[:, c, :], in0=banks[bk][:, 0:T],
                                                                    in1=fT[:, c, :], op=ALU.add),
                               reads=[bank_b[bk], B_f[c]], writes=[B_f[c]])
                        si = sq_rot.next()
                        trk.op(ACT, lambda: nc.scalar.activation(out=sq[:, si, 0:T], in_=fT[:, c, :], func=AF.Square),
                               reads=[B_f[c]], writes=[B_sq[si]])
                        pend.append((si, c))
                        if len(pend) > 1:
                            stat_mm(*pend.pop(0))
        while pend:
            stat_mm(*pend.pop(0))
        r4 = rinv[:, 1, 0:T]
        trk.op(DVE, lambda: nc.vector.tensor_tensor(out=r4, in0=banks[TR][:, 0:T], in1=r3q, op=ALU.mult),
               reads=[bank_b[TR], B_attn[1]], writes=[B_rinv[1]])
        trk.op(DVE, lambda: nc.vector.tensor_scalar(out=r4, in0=r4, scalar1=EPS, scalar2=None, op0=ALU.add),
               reads=[B_rinv[1]], writes=[B_rinv[1]])
        trk.op(ACT, lambda: nc.scalar.activation(out=r4, in_=r4, func=AF.Sqrt), reads=[B_rinv[1]], writes=[B_rinv[1]])
        trk.op(DVE, lambda: nc.vector.reciprocal(out=r4, in_=r4), reads=[B_rinv[1]], writes=[B_rinv[1]])
        trk.op(DVE, lambda: nc.vector.tensor_tensor(out=r4, in0=r4, in1=r3sq, op=ALU.mult),
               reads=[B_rinv[1], B_attn[0]], writes=[B_rinv[1]])
        for c in range(NCH):
            ti_ = tmp_rot.next()
            trk.op(DVE, lambda: nc.vector.scalar_tensor_tensor(out=tmpf[:, ti_, 0:T], in0=fT[:, c, :], scalar=gcol(G_MPOST, c),
                                                               in1=rinv[:, 1, 0:T], op0=ALU.mult, op1=ALU.mult),
                   reads=[B_f[c], B_rinv[1], B_cst], writes=[B_tmp[ti_]])
            trk.op(DVE, lambda: nc.vector.tensor_tensor(out=xT[:, c, 0:T], in0=xT[:, c, 0:T], in1=tmpf[:, ti_, 0:T],
                                                        op=ALU.add),
                   reads=[B_xT[c], B_tmp[ti_]], writes=[B_xT[c]])
        ckpt("p5")
        for tb in range(nb):
            for cg in range(4):
                bk = big_rot.next()
                for ci in range(4):
                    c = 4 * cg + ci
                    trk.op(PE, lambda ci=ci, c=c: nc.tensor.transpose(banks[bk][:, ci * 128:(ci + 1) * 128],
                                                                      xT[:, c, tb * 128:(tb + 1) * 128], identF[:]),
                           reads=[B_xT[c], B_cst], writes=[bank_b[bk]])
                evac_copy(ev_rot.next(), fbuf[:, tb * 2048 + cg * 512: tb * 2048 + (cg + 1) * 512], banks[bk][:],
                          [bank_b[bk]], [B_f[4 * tb + cg]])
            dst = ys if sample else yp[ti * 512 + tb * 128: ti * 512 + (tb + 1) * 128, :]
            trk.dma(SP, dst, fbuf[:, tb * 2048:(tb + 1) * 2048], ysem[tb], reads=B_f[4 * tb:4 * tb + 4], writes=[B_y])

    try:
        if "p0" in TILES:
            run_tile("p", 0)
        if "p1" in TILES:
            run_tile("p", 1)
        if "s" in TILES:
            trk.barrier()
            run_tile("s", 0)
            trk.barrier()
        for ti in range(2, 4):
            if f"p{ti}" in TILES:
                run_tile("p", ti)
        assert wstate["cons"] == total_blocks, (wstate, total_blocks)
    except StopBuild:
        pass
    for _ in range(PAD_DMA):
        trk.dma(POOL, wpool_sb[:], w_pool.rearrange("g (kc p) n -> p g kc n", p=128), csem_p, writes=[B_cst])
    for _ in range(PAD_PE):
        trk.op(PE, lambda: nc.tensor.matmul(banks[0][:, 0:8], lhsT=onesD[:], rhs=onesD[:, 0:8], start=True, stop=True),
               reads=[B_cst], writes=[bank_b[0]])
    trk.barrier()
    return nc


_CACHE = {}


def _get_program():
    if "nc" not in _CACHE:
        _CACHE["nc"] = build_program()
        _CACHE["cst"] = _build_consts()
    return _CACHE["nc"], _CACHE["cst"]


def kernel(x_prompt, x_sample, cache_k_win, cache_v_win, state_pool, norm_attn_pre, norm_attn_post, w_in,
           attn_sinks, w_pool, pool_scale, w_out, norm_mlp_pre, norm_mlp_post, w_up, w_down):
    nc, cst = _get_program()
    f = lambda a: np.ascontiguousarray(np.asarray(a, dtype=np.float32))
    x_prompt = f(x_prompt); x_sample = f(x_sample)
    ck = f(cache_k_win)[0].reshape(128, 128, 256)
    cv = f(cache_v_win)[0].reshape(128, 128, 256)
    sp = f(state_pool)[0]
    gl = [f(norm_attn_pre)[0], f(norm_attn_post)[0], f(pool_scale)[0], f(norm_mlp_pre)[0], f(norm_mlp_post)[0]]
    gvec = np.ascontiguousarray(np.stack([g.reshape(16, 128).T for g in gl], axis=1).reshape(128, 80))
    sk = f(attn_sinks)[0].reshape(16, 2)
    sinks = np.ascontiguousarray(np.repeat(sk.T, 64, axis=0))
    shared = {"w_in": f(w_in)[0], "w_pool": f(w_pool)[0], "w_out": f(w_out)[0], "w_up": f(w_up)[0],
              "w_down": f(w_down)[0], "gvec": gvec, "sinks": sinks, "cst": cst}
    in_maps = []
    for i in range(N_CORES):
        m = dict(shared)
        m["xp"] = x_prompt[i]
        m["xs"] = np.ascontiguousarray(x_sample[16 * i:16 * (i + 1)].reshape(128, D))
        m["ck"] = np.ascontiguousarray(ck[16 * i:16 * (i + 1)])
        m["cv"] = np.ascontiguousarray(cv[16 * i:16 * (i + 1)])
        m["spool"] = np.ascontiguousarray(sp[16 * i:16 * (i + 1)])
        in_maps.append(m)
    res = run_bass_kernel_spmd(nc, in_maps, core_ids=list(range(N_CORES)))
    r = res.results
    y_prompt = np.stack([r[i]["yp"] for i in range(N_CORES)], axis=0)
    y_sample = np.concatenate([r[i]["ys"].reshape(16, 8, D) for i in range(N_CORES)], axis=0)
    k_win_prompt = np.stack([r[i]["kp"].reshape(128, 4, 64) for i in range(N_CORES)], axis=0)[None]
    v_win_prompt = np.stack([r[i]["vp"].reshape(128, 4, 64) for i in range(N_CORES)], axis=0)[None]
    pool_prompt = np.stack([r[i]["upo"] for i in range(N_CORES)], axis=0)[None]
    k_win_sample = np.concatenate([r[i]["kso"].reshape(16, 128, 4, 64) for i in range(N_CORES)], axis=0)[None]
    v_win_sample = np.concatenate([r[i]["vso"].reshape(16, 128, 4, 64) for i in range(N_CORES)], axis=0)[None]
    pool_sample = np.concatenate([r[i]["uso"] for i in range(N_CORES)], axis=0)[None]
    return (y_prompt.astype(np.float32), y_sample.astype(np.float32), k_win_prompt.astype(np.float32),
            v_win_prompt.astype(np.float32), pool_prompt.astype(np.float32), k_win_sample.astype(np.float32),
            v_win_sample.astype(np.float32), pool_sample.astype(np.float32))
```

```python
import numpy as np
import concourse.bass as bass
import concourse.mybir as mybir
from concourse.bass_utils import run_bass_kernel_spmd

F32 = mybir.dt.float32
BF16 = mybir.dt.bfloat16
AF = mybir.ActivationFunctionType
ALU = mybir.AluOpType

D = 2048
NCH = 16
SEQ = 2048
DEC_B = 16
DEC_T = 8
HD = 64
NKV = 4
D_IN = 7680
D_FF = 8192
POOL_W = (2, 4, 8, 16)
EPS = 1e-6
N_CORES = 8

SAME_ENGINE_SYNC = True
TILES = ["s", "p0", "p1", "p2", "p3"]
STOP = None
PAD_PE = 0
PAD_DMA = 0
PREFETCH_X = False
NS_SLOTS = 3
USE_SCRATCH = True


class StopBuild(Exception):
    pass


def ckpt(label):
    if STOP == label:
        raise StopBuild(label)


class Buf:
    __slots__ = ("name", "w", "r", "excl")

    def __init__(self, name, excl=False):
        self.name = name
        self.w = None
        self.r = []
        self.excl = excl


class DmaSem:
    _n = 0

    def __init__(self, nc, name):
        self.sem = nc.alloc_semaphore(name)
        self.key = f"dma_{name}_{DmaSem._n}"
        DmaSem._n += 1
        self.val = 0


class Eng:
    def __init__(self, name, h, sem):
        self.name = name
        self.h = h
        self.sem = sem
        self.pos = 0
        self.last = None
        self.last_has_sig = False
        self.nsig = 0
        self.sig_pos = 0
        self.sigs = []
        self.waited = {}

    def wait_sem(self, sem, key, val):
        if self.waited.get(key, 0) >= val:
            return
        self.presignal()
        self.h.wait_ge(sem, val)
        self.waited[key] = val
        self.pos += 1
        self.last = None
        self.last_has_sig = False

    def presignal(self):
        if self.last is not None and not self.last_has_sig and self.sig_pos < self.pos:
            self.signal_for(self.pos)

    def signal_for(self, pos):
        if self.sig_pos >= pos:
            for p, c in self.sigs:
                if p >= pos:
                    return c
            raise AssertionError
        assert self.last is not None and not self.last_has_sig, (self.name, pos, self.pos)
        self.last.then_inc(self.sem, 1)
        self.last_has_sig = True
        self.nsig += 1
        self.sig_pos = self.pos
        self.sigs.append((self.pos, self.nsig))
        return self.nsig


class Trk:
    def __init__(self, nc):
        self.nc = nc
        self.engs = {}
        self.dsems = []

    def add_engine(self, name, h):
        e = Eng(name, h, self.nc.alloc_semaphore(f"s_{name}"))
        self.engs[name] = e
        return e

    def dsem(self, name):
        d = DmaSem(self.nc, name)
        self.dsems.append(d)
        return d

    def _wait(self, e, ev):
        if ev[0] == "c":
            _, pe, pos = ev
            if pe is e:
                if not SAME_ENGINE_SYNC or e.name == "pe":
                    return
                if e.last is None and e.sig_pos < pos:
                    return
            cnt = pe.signal_for(pos)
            e.wait_sem(pe.sem, pe.name, cnt)
        else:
            _, d, val = ev
            e.wait_sem(d.sem, d.key, max(val, d.val))

    def _deps(self, e, reads, writes):
        deps = []
        for b in reads:
            if b.w is not None:
                deps.append(b.w)
            if b.excl:
                deps.extend(ev for ev in b.r if ev[0] == "c" and ev[1] is not e)
        for b in writes:
            if b.w is not None:
                deps.append(b.w)
            deps.extend(b.r)
        best = {}
        for ev in deps:
            if ev[0] == "c":
                k = ("c", ev[1].name)
                if k not in best or best[k][2] < ev[2]:
                    best[k] = ev
            else:
                k = ("d", ev[1].key)
                if k not in best or best[k][2] < ev[2]:
                    best[k] = ev
        for ev in best.values():
            self._wait(e, ev)

    def _record(self, ev, reads, writes):
        for b in reads:
            b.r.append(ev)
        for b in writes:
            b.w = ev
            b.r = []

    def op(self, ename, fn, reads=(), writes=()):
        e = self.engs[ename]
        self._deps(e, reads, writes)
        ins = fn()
        e.pos += 1
        e.last = ins
        e.last_has_sig = False
        self._record(("c", e, e.pos), reads, writes)
        return ins

    def dma(self, qname, out, in_, dsem, reads=(), writes=(), **kw):
        e = self.engs[qname]
        self._deps(e, reads, writes)
        e.presignal()
        ins = e.h.dma_start(out=out, in_=in_, **kw)
        dsem.val += 16
        ins.then_inc(dsem.sem, 16)
        e.pos += 1
        e.last = None
        e.last_has_sig = False
        self._record(("d", dsem, dsem.val), reads, writes)
        return ins

    def barrier(self):
        cnts = {}
        for n, e in self.engs.items():
            if e.pos > 0 and e.sig_pos < e.pos and e.last is not None:
                e.signal_for(e.pos)
            cnts[n] = e.nsig
        for n, e in self.engs.items():
            for n2, e2 in self.engs.items():
                if n2 != n and cnts[n2] > 0:
                    e.wait_sem(e2.sem, e2.name, cnts[n2])
            for d in self.dsems:
                if d.val > 0:
                    e.wait_sem(d.sem, d.key, d.val)


class Rot:
    def __init__(self, items):
        self.items = items
        self.i = 0

    def next(self):
        it = self.items[self.i % len(self.items)]
        self.i += 1
        return it


C_MASK = 0
C_POOL = 1536
C_HIST = C_POOL + 16 * 128
C_BF_END = C_HIST + 4 * 64
C_ID = C_BF_END
C_END = C_ID + 128


def _build_consts():
    c = np.zeros((128, C_END), np.float32)
    j = np.arange(128)[:, None]
    i = np.arange(128)[None, :]
    prev = (j > i).astype(np.float32)
    cur = (j <= i).astype(np.float32)
    M0 = C_MASK
    for b2 in range(2):
        c[:, M0 + (b2 * 2 + 0) * 128: M0 + (b2 * 2 + 1) * 128] = prev
        c[:, M0 + (b2 * 2 + 1) * 128: M0 + (b2 * 2 + 2) * 128] = cur
    M1 = C_MASK + 512
    c[:, M1 + 128: M1 + 256] = cur
    c[:, M1 + 256: M1 + 384] = prev
    c[:, M1 + 384: M1 + 512] = cur
    t_of = np.arange(128) % 8
    b_of = np.arange(128) // 8
    hist = (np.arange(128)[:, None] > t_of[None, :]).astype(np.float32)
    scur = ((b_of[:, None] == b_of[None, :]) & (t_of[:, None] <= t_of[None, :])).astype(np.float32)
    M2 = C_MASK + 1024
    c[:, M2: M2 + 128] = hist
    c[:, M2 + 128: M2 + 256] = scur
    tp = np.arange(128)[:, None]
    t = np.arange(128)[None, :]
    eye = (tp == t).astype(np.float32)
    for g, w in enumerate(POOL_W):
        inw = ((t - tp >= 0) & (t - tp < w)).astype(np.float32)
        c[:, C_POOL + (0 + g) * 128: C_POOL + (1 + g) * 128] = inw / w - eye
        c[:, C_POOL + (4 + g) * 128: C_POOL + (5 + g) * 128] = ((t + 128 - tp) < w).astype(np.float32) / w
        cnt = np.minimum(t + 1, w).astype(np.float32)
        c[:, C_POOL + (8 + g) * 128: C_POOL + (9 + g) * 128] = inw / cnt - eye
        sinw = ((b_of[:, None] == b_of[None, :]) & (t_of[None, :] - t_of[:, None] >= 0)
                & (t_of[None, :] - t_of[:, None] < w)).astype(np.float32)
        c[:, C_POOL + (12 + g) * 128: C_POOL + (13 + g) * 128] = sinw / w - eye
        hb = np.arange(120) // 15
        ht = np.arange(120) % 15
        cb = np.arange(64) // 8
        cs = np.arange(64) % 8
        hm = ((hb[:, None] == cb[None, :]) & (ht[:, None] >= 16 + cs[None, :] - w)).astype(np.float32) / w
        c[:120, C_HIST + g * 64: C_HIST + (g + 1) * 64] = hm
    c[:, C_ID:C_ID + 128] = np.eye(128, dtype=np.float32)
    return c


def build_program():
    nc = bass.Bass("TRN2", target_bir_lowering=False)
    dt = lambda name, shape, kind: nc.dram_tensor(name, shape, F32, kind=kind).ap()
    xp = dt("xp", [SEQ, D], "ExternalInput")
    xs = dt("xs", [128, D], "ExternalInput")
    ck = dt("ck", [DEC_B, 128, 256], "ExternalInput")
    cv = dt("cv", [DEC_B, 128, 256], "ExternalInput")
    spool = dt("spool", [DEC_B, 15, 1024], "ExternalInput")
    w_in = dt("w_in", [D, D_IN], "ExternalInput")
    w_pool = dt("w_pool", [4, 256, 512], "ExternalInput")
    w_out = dt("w_out", [D, D], "ExternalInput")
    w_up = dt("w_up", [D, D_FF], "ExternalInput")
    w_down = dt("w_down", [D_FF, D], "ExternalInput")
    gv_d = dt("gvec", [128, 5 * 16], "ExternalInput")
    sk_d = dt("sinks", [128, 16], "ExternalInput")
    cst_d = dt("cst", [128, C_END], "ExternalInput")
    yp = dt("yp", [SEQ, D], "ExternalOutput")
    ys = dt("ys", [128, D], "ExternalOutput")
    kp = dt("kp", [128, 256], "ExternalOutput")
    vp = dt("vp", [128, 256], "ExternalOutput")
    upo = dt("upo", [15, 1024], "ExternalOutput")
    kso = dt("kso", [DEC_B, 128, 256], "ExternalOutput")
    vso = dt("vso", [DEC_B, 128, 256], "ExternalOutput")
    uso = dt("uso", [DEC_B, 15, 1024], "ExternalOutput")

    w_in_v = w_in.rearrange("(kc p) n -> p kc n", p=128)
    w_out_v = w_out.rearrange("(kc p) n -> p kc n", p=128)
    w_up_v = w_up.rearrange("(kc p) n -> p kc n", p=128)
    w_down_v = w_down.rearrange("(kc p) n -> p kc n", p=128)

    trk = Trk(nc)
    trk.add_engine("pe", nc.tensor)
    trk.add_engine("act", nc.scalar)
    trk.add_engine("dve", nc.vector)
    trk.add_engine("pool", nc.gpsimd)
    trk.add_engine("sp", nc.sync)
    PE, ACT, DVE, POOL, SP = "pe", "act", "dve", "pool", "sp"

    sb = lambda name, shape, dtype: nc.alloc_sbuf_tensor(name, shape, dtype)
    xT = sb("xT", [128, NCH, 512], F32)
    hbuf = sb("hbuf", [128, 8192], BF16)
    mixb = sb("mixb", [128, 8192], BF16)
    fbuf = sb("fbuf", [128, 8192], F32)
    arena = sb("arena", [128, 16384], BF16)
    wsl = sb("wsl", [128, NS_SLOTS, 4096], BF16)
    sg = sb("sg", [128, 4, 512], F32)
    attn = sb("attn", [128, 4, 512], F32)
    sq = sb("sq", [128, 2, 512], BF16)
    rinv = sb("rinv", [128, 2, 512], F32)
    tmpf = sb("tmpf", [128, 2, 512], F32)
    wpool_sb = sb("wpool_sb", [128, 4, 2, 512], BF16)
    cstb = sb("cstb", [128, C_BF_END], BF16)
    identF = sb("identF", [128, 128], F32)
    onesD = sb("onesD", [128, 128], BF16)
    ones64 = sb("ones64", [128, 64], BF16)
    gvec = sb("gvec_sb", [128, 5, 16], F32)
    esink = sb("esink", [128, 16], F32)
    kprev = sb("kprev", [128, 4, 128], BF16)
    pT_sb = sb("pT_sb", [128, 8, 512], BF16)
    vprev = sb("vprev", [128, 256], BF16)
    uprev = sb("uprev", [128, 1024], BF16)

    banks = [nc.alloc_psum_tensor(f"bank{i}", [128, 512], F32) for i in range(8)]
    bank_b = [Buf(f"bank{i}", excl=True) for i in range(8)]
    mm_rot = Rot([0, 1])
    big_rot = Rot([0, 1, 2, 3, 4, 5])
    sc_rot = Rot([(2, 3), (4, 5)])
    AV, DEN, TR = 6, 7, 7

    B_xT = [Buf(f"xT{c}") for c in range(NCH)]
    B_h = [Buf(f"h{c}") for c in range(NCH)]
    B_mix = [Buf(f"mix{c}") for c in range(NCH)]
    B_f = [Buf(f"f{c}") for c in range(NCH)]
    B_up = [Buf(f"up{c}") for c in range(32)]
    B_q = [Buf("q0"), Buf("q1")]
    B_kT2 = Buf("kT2")
    B_v = [Buf(f"v{i}") for i in range(5)]
    B_u = [Buf(f"u{i}") for i in range(5)]
    B_pl = [Buf(f"pl{i}") for i in range(8)]
    B_pT = [Buf(f"pT{i}") for i in range(8)]
    B_ws = [Buf(f"ws{i}") for i in range(4)]
    B_sg = [Buf(f"sg{i}") for i in range(4)]
    B_attn = [Buf(f"attn{i}") for i in range(4)]
    B_sq = [Buf("sq0"), Buf("sq1")]
    B_rinv = [Buf("rinv0"), Buf("rinv1")]
    B_tmp = [Buf("tmp0"), Buf("tmp1")]
    B_cst = Buf("cst")
    B_out = Buf("out_small")
    B_y = Buf("y_out")
    B_khT = Buf("khT")
    B_vh = Buf("vhist")
    B_uh = Buf("uh")
    B_prev = Buf("prev")
    B_sst = [Buf(f"sst{i}") for i in range(6)]
    B_ust = [Buf(f"ust{i}") for i in range(4)]
    B_kvst = [Buf("kst"), Buf("vst")]
    SG_ALIAS = {0: [B_ust[0], B_ust[1]], 1: [B_ust[2], B_ust[3]], 2: B_kvst, 3: []}

    wsem = [trk.dsem(f"w{i}") for i in range(4)]
    ssem = [trk.dsem(f"ss{i}") for i in range(4)]
    xsem = [trk.dsem(f"x{i}") for i in range(4)]
    ysem = [trk.dsem(f"y{i}") for i in range(4)]
    csem = trk.dsem("cst")
    csem_p = trk.dsem("cstp")
    hsem_p = trk.dsem("histp")
    osem = trk.dsem("osm")
    osem_kv = [trk.dsem("okv0"), trk.dsem("okv1")]
    osem_u = [trk.dsem(f"ou{i}") for i in range(4)]
    hsem = trk.dsem("hist")

    sq_rot = Rot([0, 1])
    tmp_rot = Rot([0, 1])
    pT_rot = Rot(list(range(8)))
    sga_rot = Rot([0, 1])
    sgp_rot = Rot([2, 3])
    ev_rot = Rot([DVE, ACT])

    trk.dma(SP, identF[:], cst_d[:, C_ID:C_END], csem, writes=[B_cst])
    trk.dma(SP, gvec[:], gv_d.rearrange("p (w c) -> p w c", w=5), csem, writes=[B_cst])
    trk.dma(SP, esink[:], sk_d, csem, writes=[B_cst])
    trk.dma(POOL, cstb[:], cst_d[:, 0:C_BF_END], csem_p, writes=[B_cst])
    trk.dma(POOL, wpool_sb[:], w_pool.rearrange("g (kc p) n -> p g kc n", p=128), csem_p, writes=[B_cst])
    trk.op(DVE, lambda: nc.vector.memset(onesD[:], 1.0 / D), writes=[B_cst])
    trk.op(DVE, lambda: nc.vector.memset(ones64[:], 1.0), writes=[B_cst])
    trk.op(ACT, lambda: nc.scalar.activation(out=esink[:], in_=esink[:], func=AF.Exp), reads=[B_cst], writes=[B_cst])

    G_PRE, G_POST, G_PSC, G_MPRE, G_MPOST = 0, 1, 2, 3, 4

    def gcol(which, c):
        return gvec[:, which, c:c + 1]

    NS = NS_SLOTS

    def tile_blocks():
        bl = []
        for part in range(2):
            bl.append((f"kv{part}", w_in_v[:, :, 2048 + part * 256:2048 + (part + 1) * 256], 16, 256))
        for ub in range(4):
            bl.append((f"u{ub}", w_in_v[:, :, 2560 + ub * 256:2560 + (ub + 1) * 256], 16, 256))
        for g in range(4):
            for nm, base in (("q", 0), ("ga", 3584), ("gp", 5632)):
                for hb in range(2):
                    c0 = base + g * 512 + hb * 256
                    bl.append((f"{nm}{g}_{hb}", w_in_v[:, :, c0:c0 + 256], 16, 256))
        for j in range(8):
            bl.append((f"o{j}", w_out_v[:, :, j * 256:(j + 1) * 256], 16, 256))
        for half in range(2):
            for j in range(16):
                jj = half * 16 + j
                bl.append((f"up{jj}", w_up_v[:, :, jj * 256:(jj + 1) * 256], 16, 256))
            for cp in range(8):
                for kq in range(2):
                    k0 = half * 32 + kq * 16
                    bl.append((f"dn{half}_{cp}_{kq}", w_down_v[:, k0:k0 + 16, cp * 256:(cp + 1) * 256], 16, 256))
        return bl

    blocks1 = tile_blocks()
    NBLK = len(blocks1)
    N_TILES = len(TILES)
    wscr = nc.dram_tensor("wscr", [NBLK, 128, 4096], BF16, kind="Internal").ap()
    B_scr = [Buf(f"scr{j}") for j in range(NBLK)]
    wstate = {"issued": 0, "cons": 0}
    total_blocks = NBLK * N_TILES

    def w_issue():
        i = wstate["issued"]
        if i >= total_blocks:
            return
        j = i % NBLK
        name, src, kcn, ncol = blocks1[j]
        s_ = i % NS
        dst = wsl[:, s_, 0:kcn * ncol].rearrange("p (k n) -> p k n", k=kcn)
        tix = i // NBLK
        J0 = NBLK // 2 if N_TILES > 2 else NBLK
        convert_here = (tix == 0 and j < J0) or (tix == 1 and j >= J0)
        from_fp32 = (tix == 0) or (tix == 1 and j >= J0) or not USE_SCRATCH
        if from_fp32:
            trk.dma(POOL, dst, src, wsem[s_], writes=[B_ws[s_]])
            if N_TILES > 1 and USE_SCRATCH and convert_here:
                trk.dma(SP, wscr[j], wsl[:, s_, :], ssem[s_], reads=[B_ws[s_]], writes=[B_scr[j]])
        else:
            trk.dma(POOL, dst, wscr[j].rearrange("p (k n) -> p k n", k=kcn), wsem[s_], reads=[B_scr[j]],
                    writes=[B_ws[s_]])
        wstate["issued"] += 1

    def w_next(expect):
        i = wstate["cons"]
        name, src, kcn, ncol = blocks1[i % NBLK]
        assert name == expect, (name, expect)
        while wstate["issued"] <= i:
            w_issue()
        s_ = i % NS
        wstate["cons"] += 1
        view = wsl[:, s_, 0:kcn * ncol].rearrange("p (k n) -> p k n", k=kcn)
        return view, B_ws[s_]

    def w_done():
        while wstate["issued"] < min(total_blocks, wstate["cons"] + NS):
            w_issue()

    for _ in range(NS):
        w_issue()

    def stats_rinv(ri):
        r = rinv[:, ri, :]
        trk.op(DVE, lambda: nc.vector.tensor_scalar(out=r[:, 0:TT[0]], in0=banks[TR][:, 0:TT[0]], scalar1=EPS,
                                                    scalar2=None, op0=ALU.add),
               reads=[bank_b[TR]], writes=[B_rinv[ri]])
        trk.op(ACT, lambda: nc.scalar.activation(out=r[:, 0:TT[0]], in_=r[:, 0:TT[0]], func=AF.Sqrt),
               reads=[B_rinv[ri]], writes=[B_rinv[ri]])
        trk.op(DVE, lambda: nc.vector.reciprocal(out=r[:, 0:TT[0]], in_=r[:, 0:TT[0]]),
               reads=[B_rinv[ri]], writes=[B_rinv[ri]])

    TT = [512]
    pf_state = {}
    NEXT_PROMPT = {0: 1, 2: 3}

    def stat_mm(si, c):
        T = TT[0]
        trk.op(PE, lambda: nc.tensor.matmul(banks[TR][:, 0:T], lhsT=onesD[:], rhs=sq[:, si, 0:T],
                                            start=(c == 0), stop=(c == NCH - 1)),
               reads=[B_sq[si], B_cst], writes=[bank_b[TR]])

    def evac_copy(eng, out, in_, reads, writes, scale=None):
        if eng == DVE:
            if scale is None:
                trk.op(DVE, lambda: nc.vector.tensor_copy(out=out, in_=in_), reads=reads, writes=writes)
            else:
                trk.op(DVE, lambda: nc.vector.tensor_scalar(out=out, in0=in_, scalar1=scale, scalar2=None,
                                                            op0=ALU.mult), reads=reads, writes=writes)
        else:
            if scale is None:
                trk.op(ACT, lambda: nc.scalar.activation(out=out, in_=in_, func=AF.Copy), reads=reads, writes=writes)
            else:
                trk.op(ACT, lambda: nc.scalar.activation(out=out, in_=in_, func=AF.Copy, scale=scale),
                       reads=reads, writes=writes)

    def run_tile(kind, ti):
        sample = (kind == "s")
        T = 128 if sample else 512
        TT[0] = T
        nb = T // 128
        first = (not sample) and ti == 0
        last = (not sample) and ti == 3
        hT = hbuf[:, 0:NCH * T].rearrange("p (c t) -> p c t", c=NCH)
        mixT = mixb[:, 0:NCH * T].rearrange("p (c t) -> p c t", c=NCH)
        fT = fbuf[:, 0:NCH * T].rearrange("p (c t) -> p c t", c=NCH)
        a0 = 0
        upT = arena[:, 0:32 * T].rearrange("p (c t) -> p c t", c=32)
        qT = arena[:, a0:a0 + 2 * T].rearrange("p (c t) -> p c t", c=2); a0 += 2 * T
        KW = T + 128
        kT2 = arena[:, a0:a0 + 4 * KW].rearrange("p (h t) -> p h t", h=4); a0 += 4 * KW
        vbuf = arena[:, a0:a0 + (nb + 1) * 256].rearrange("p (b c) -> p b c", c=256); a0 += (nb + 1) * 256
        ubuf = arena[:, a0:a0 + (nb + 1) * 1024].rearrange("p (b c) -> p b c", c=1024); a0 += (nb + 1) * 1024
        pooledT = arena[:, a0:a0 + 8 * T].rearrange("p (c t) -> p c t", c=8); a0 += 8 * T
        pT = pT_sb
        assert a0 <= 16384 and (not sample or a0 <= 8192)
        if sample:
            khT = arena[:, 8192:16384].rearrange("p (b h k) -> p b h k", b=16, h=4)
            vhist = mixb[:, 2048:2048 + 4096].rearrange("p (b c) -> p b c", b=16)
            uh = mixb[:, 6144:8192].rearrange("p (g c) -> p g c", g=2)

        ckpt("pre")
        prefetched = (not sample) and PREFETCH_X and pf_state.get("tile") == ti
        mixF = mixb[:].bitcast(F32)
        hF = hbuf[:].bitcast(F32)

        def x_stage(tb, c):
            if not prefetched:
                return fbuf[:, tb * 2048 + c * 128: tb * 2048 + (c + 1) * 128], B_f[4 * tb + c // 4]
            src_t, bl = (mixF, B_mix) if tb < 2 else (hF, B_h)
            o = (tb % 2) * 2048 + c * 128
            return src_t[:, o:o + 128], bl[8 * (tb % 2) + c // 2]

        if not prefetched:
            for tb in range(nb):
                src = xs if sample else xp[ti * 512 + tb * 128: ti * 512 + (tb + 1) * 128, :]
                trk.dma(SP, fbuf[:, tb * 2048:(tb + 1) * 2048], src, xsem[tb], writes=B_f[4 * tb:4 * tb + 4])
        if sample:
            trk.dma(POOL, vhist, cv.rearrange("b k c -> k b c"), hsem_p, writes=[B_vh])
            trk.dma(POOL, uh[0:120], spool.rearrange("(g b) t c -> (b t) g c", g=2), hsem_p, writes=[B_uh])
            trk.dma(SP, kso[:, 0:120, :], ck[:, 8:128, :], osem, writes=[B_out])
            trk.dma(SP, vso[:, 0:120, :], cv[:, 8:128, :], osem, writes=[B_out])
            trk.dma(SP, uso[:, 0:7, :], spool[:, 8:15, :], osem, writes=[B_out])
        ckpt("loads")
        if first:
            trk.op(DVE, lambda: nc.vector.memset(kT2[:, :, 0:128], 0.0), writes=[B_kT2])
            trk.op(DVE, lambda: nc.vector.memset(vbuf[:, 0, :], 0.0), writes=[B_v[0]])
            trk.op(DVE, lambda: nc.vector.memset(ubuf[:, 0, :], 0.0), writes=[B_u[0]])
        elif not sample:
            trk.op(DVE, lambda: nc.vector.tensor_copy(out=kT2[:, :, 0:128], in_=kprev[:]), reads=[B_prev],
                   writes=[B_kT2] + B_up)
            trk.op(DVE, lambda: nc.vector.tensor_copy(out=vbuf[:, 0, :], in_=vprev[:]), reads=[B_prev], writes=[B_v[0]])
            trk.op(DVE, lambda: nc.vector.tensor_copy(out=ubuf[:, 0, :], in_=uprev[:]), reads=[B_prev], writes=[B_u[0]])

        pend = []
        for c in range(NCH):
            bk = big_rot.next()
            for tb in range(nb):
                xs_ap, xs_b = x_stage(tb, c)
                trk.op(PE, lambda tb=tb: nc.tensor.transpose(banks[bk][:, tb * 128:(tb + 1) * 128], xs_ap, identF[:]),
                       reads=[xs_b, B_cst], writes=[bank_b[bk]])
            si = sq_rot.next()
            trk.op(DVE, lambda: nc.vector.tensor_copy(out=xT[:, c, 0:T], in_=banks[bk][:, 0:T]),
                   reads=[bank_b[bk]], writes=[B_xT[c]])
            trk.op(ACT, lambda: nc.scalar.activation(out=sq[:, si, 0:T], in_=xT[:, c, 0:T], func=AF.Square),
                   reads=[B_xT[c]], writes=[B_sq[si]])
            pend.append((si, c))
            if len(pend) > 1:
                stat_mm(*pend.pop(0))
        while pend:
            stat_mm(*pend.pop(0))
        ckpt("p1t")
        stats_rinv(0)
        ckpt("p1r")
        for c in range(NCH):
            trk.op(DVE, lambda: nc.vector.scalar_tensor_tensor(out=hT[:, c, :], in0=xT[:, c, 0:T], scalar=gcol(G_PRE, c),
                                                               in1=rinv[:, 0, 0:T], op0=ALU.mult, op1=ALU.mult),
                   reads=[B_xT[c], B_rinv[0], B_cst], writes=[B_h[c]])

        ckpt("p1")
        if sample:
            stg8 = attn[:].rearrange("p a (b c) -> p (a b) c", b=2)
            for r in range(2):
                trk.dma(SP, stg8, ck[8 * r:8 * r + 8].rearrange("b k c -> k b c"), hsem, writes=B_attn)
                for b8 in range(8):
                    ti_ = tmp_rot.next()
                    kd = tmpf[:, ti_, :].rearrange("p (h e d) -> p h e d", h=4, e=2)
                    ksrc = stg8[:, b8, :].rearrange("p (h d) -> p h d", h=4)
                    for e in range(2):
                        trk.op(DVE, lambda e=e: nc.vector.tensor_copy(out=kd[:, :, e, :], in_=ksrc),
                               reads=[B_attn[b8 // 2]], writes=[B_tmp[ti_]])
                    bk = big_rot.next()
                    for h in range(4):
                        trk.op(PE, lambda h=h: nc.tensor.transpose(banks[bk][:, h * 128:(h + 1) * 128],
                                                                   tmpf[:, ti_, h * 128:(h + 1) * 128], identF[:]),
                               reads=[B_tmp[ti_], B_cst], writes=[bank_b[bk]])
                    evac_copy(ACT, khT[:, 8 * r + b8, :, :],
                              banks[bk][:].rearrange("p (h k) -> p h k", h=4), [bank_b[bk]], [B_khT])

        ckpt("hist")
        for part in range(2):
            wv, wb = w_next(f"kv{part}")
            for tb in range(nb):
                bk = mm_rot.next()
                for kc in range(NCH):
                    trk.op(PE, lambda kc=kc: nc.tensor.matmul(banks[bk][:, 0:256], lhsT=hT[:, kc, tb * 128:(tb + 1) * 128],
                                                              rhs=wv[:, kc, :], start=(kc == 0), stop=(kc == NCH - 1)),
                           reads=[B_h[kc], wb], writes=[bank_b[bk]])
                stg_out = sample or (last and tb == nb - 1)
                if stg_out:
                    if sample:
                        so = 4096 + part * 256
                        trk.op(DVE, lambda: nc.vector.tensor_copy(out=fbuf[:, so:so + 256], in_=banks[bk][:, 0:256]),
                               reads=[bank_b[bk]], writes=[B_sst[part]])
                        dst_o = kso if part == 0 else vso
                        for bt in range(DEC_B):
                            trk.dma(SP, dst_o[bt, 120:128, :], fbuf[bt * 8:(bt + 1) * 8, so:so + 256],
                                    osem_kv[part], reads=[B_sst[part]], writes=[B_out])
                        continue_stage = False
                    else:
                        trk.op(DVE, lambda: nc.vector.tensor_copy(out=sg[:, 2, part * 256:(part + 1) * 256],
                                                                  in_=banks[bk][:, 0:256]),
                               reads=[bank_b[bk], B_sg[2]], writes=[B_kvst[part]])
                    if sample:
                        pass
                    else:
                        trk.dma(SP, kp if part == 0 else vp, sg[:, 2, part * 256:(part + 1) * 256], osem_kv[part],
                                reads=[B_kvst[part]], writes=[B_out])
                if part == 1:
                    evac_copy(DVE if stg_out else ev_rot.next(), vbuf[:, 1 + tb, :], banks[bk][:, 0:256], [bank_b[bk]],
                              [B_v[1 + tb]])
                    continue
                kd = attn[:, tb % 2, :].rearrange("p (h e d) -> p h e d", h=4, e=2)
                ksrc = banks[bk][:, 0:256].rearrange("p (h d) -> p h d", h=4)
                trk.op(DVE, lambda: nc.vector.tensor_copy(out=kd[:, :, 0, :], in_=ksrc), reads=[bank_b[bk]],
                       writes=[B_attn[tb % 2]])
                trk.op(DVE, lambda: nc.vector.tensor_copy(out=kd[:, :, 1, :], in_=ksrc), reads=[bank_b[bk]],
                       writes=[B_attn[tb % 2]])
                for h in range(4):
                    trk.op(PE, lambda h=h: nc.tensor.transpose(banks[TR][:, h * 128:(h + 1) * 128],
                                                               attn[:, tb % 2, h * 128:(h + 1) * 128], identF[:]),
                           reads=[B_attn[tb % 2], B_cst], writes=[bank_b[TR]])
                trk.op(DVE, lambda: nc.vector.tensor_copy(out=kT2[:, :, 128 + tb * 128:128 + (tb + 1) * 128],
                                                          in_=banks[TR][:].rearrange("p (h k) -> p h k", h=4)),
                       reads=[bank_b[TR]], writes=[B_kT2])
            w_done()

        ckpt("p2a")
        sgf = sg[:].rearrange("p a n -> p (a n)")
        for ub in range(4):
            wv, wb = w_next(f"u{ub}")
            for tb in range(nb):
                bk = mm_rot.next()
                for kc in range(NCH):
                    trk.op(PE, lambda kc=kc: nc.tensor.matmul(banks[bk][:, 0:256], lhsT=hT[:, kc, tb * 128:(tb + 1) * 128],
                                                              rhs=wv[:, kc, :], start=(kc == 0), stop=(kc == NCH - 1)),
                           reads=[B_h[kc], wb], writes=[bank_b[bk]])
                stg_out = sample or (last and tb == nb - 1)
                evac_copy(DVE if stg_out else ev_rot.next(), ubuf[:, 1 + tb, ub * 256:(ub + 1) * 256],
                          banks[bk][:, 0:256], [bank_b[bk]], [B_u[1 + tb]])
                if stg_out:
                    if sample:
                        so = 4096 + 512 + ub * 256
                        trk.op(DVE, lambda: nc.vector.tensor_copy(out=fbuf[:, so:so + 256], in_=banks[bk][:, 0:256]),
                               reads=[bank_b[bk]], writes=[B_sst[2 + ub]])
                        for bt in range(DEC_B):
                            trk.dma(SP, uso[bt, 7:15, ub * 256:(ub + 1) * 256], fbuf[bt * 8:(bt + 1) * 8, so:so + 256],
                                    osem_u[ub], reads=[B_sst[2 + ub]], writes=[B_out])
                    else:
                        trk.op(DVE, lambda: nc.vector.tensor_copy(out=sgf[:, ub * 256:(ub + 1) * 256],
                                                                  in_=banks[bk][:, 0:256]),
                               reads=[bank_b[bk], B_sg[ub // 2]], writes=[B_ust[ub]])
                        trk.dma(SP, upo[:, ub * 256:(ub + 1) * 256], sgf[113:128, ub * 256:(ub + 1) * 256], osem_u[ub],
                                reads=[B_ust[ub]], writes=[B_out])
            w_done()
        cp = lambda idx: cstb[:, C_POOL + idx * 128: C_POOL + (idx + 1) * 128]
        for uc in range(8):
            g = uc // 2
            bk = mm_rot.next()
            if sample:
                trk.op(PE, lambda: nc.tensor.matmul(banks[bk][:, 0:128], lhsT=ubuf[:, 1, uc * 128:(uc + 1) * 128],
                                                    rhs=cp(12 + g), start=True, stop=False),
                       reads=[B_u[1], B_cst], writes=[bank_b[bk]])
                for gr in range(2):
                    trk.op(PE, lambda gr=gr: nc.tensor.matmul(banks[bk][:, gr * 64:(gr + 1) * 64],
                                                              lhsT=uh[0:120, gr, uc * 128:(uc + 1) * 128],
                                                              rhs=cstb[0:120, C_HIST + g * 64:C_HIST + (g + 1) * 64],
                                                              start=False, stop=(gr == 1)),
                           reads=[B_uh, B_cst], writes=[bank_b[bk]])
            else:
                for tb in range(nb):
                    fb = first and tb == 0
                    if not fb:
                        trk.op(PE, lambda tb=tb: nc.tensor.matmul(banks[bk][:, tb * 128:(tb + 1) * 128],
                                                                  lhsT=ubuf[:, tb, uc * 128:(uc + 1) * 128],
                                                                  rhs=cp(4 + g), start=True, stop=False),
                               reads=[B_u[tb], B_cst], writes=[bank_b[bk]])
                    trk.op(PE, lambda tb=tb, fb=fb: nc.tensor.matmul(banks[bk][:, tb * 128:(tb + 1) * 128],
                                                                     lhsT=ubuf[:, 1 + tb, uc * 128:(uc + 1) * 128],
                                                                     rhs=cp(8 + g) if fb else cp(g),
                                                                     start=fb, stop=True),
                           reads=[B_u[1 + tb], B_cst], writes=[bank_b[bk]])
            evac_copy(ev_rot.next(), pooledT[:, uc, :], banks[bk][:, 0:T], [bank_b[bk]], [B_pl[uc]])

        ckpt("p2b")
        for g in range(4):
            qw = {}

            def q_gemm(ci, g=g):
                c = 4 * g + ci
                if ci % 2 == 0:
                    qw["w"] = w_next(f"q{g}_{ci // 2}")
                wv, wb = qw["w"]
                cl = ci % 2
                bk = mm_rot.next()
                for kc in range(NCH):
                    trk.op(PE, lambda kc=kc: nc.tensor.matmul(banks[bk][:, 0:T], lhsT=wv[:, kc, cl * 128:(cl + 1) * 128],
                                                              rhs=hT[:, kc, :], start=(kc == 0), stop=(kc == NCH - 1)),
                           reads=[B_h[kc], wb], writes=[bank_b[bk]])
                qi = c % 2
                evac_copy(ACT, qT[:, qi, :], banks[bk][:, 0:T], [bank_b[bk]], [B_q[qi]], scale=0.125)
                if ci % 2 == 1:
                    w_done()

            def scores(ci, bp, g=g):
                c = 4 * g + ci
                qi = c % 2
                scp = sc_rot.next()
                pis = []
                for e in range(2):
                    sc = scp[e]
                    pr = slice(64 * e, 64 * e + 64)
                    if sample:
                        for bt in range(DEC_B):
                            trk.op(PE, lambda bt=bt: nc.tensor.matmul(
                                banks[sc][:, bt * 8:bt * 8 + 8],
                                lhsT=khT[pr, bt, g, :], rhs=qT[pr, qi, bt * 8:bt * 8 + 8], start=True, stop=True),
                                reads=[B_khT, B_q[qi]], writes=[bank_b[sc]])
                        trk.op(PE, lambda: nc.tensor.matmul(banks[sc][:, 128:256],
                                                            lhsT=kT2[pr, g, 128:256], rhs=qT[pr, qi, 0:128],
                                                            start=True, stop=True),
                               reads=[B_kT2, B_q[qi]], writes=[bank_b[sc]])
                    else:
                        for b2 in range(2):
                            b = 2 * bp + b2
                            for kb in range(2):
                                trk.op(PE, lambda kb=kb, b=b, b2=b2: nc.tensor.matmul(
                                    banks[sc][:, (b2 * 2 + kb) * 128:(b2 * 2 + kb + 1) * 128],
                                    lhsT=kT2[pr, g, (b + kb) * 128:(b + kb + 1) * 128],
                                    rhs=qT[pr, qi, b * 128:(b + 1) * 128], start=True, stop=True),
                                    reads=[B_kT2, B_q[qi]], writes=[bank_b[sc]])
                    W = 256 if sample else 512
                    pi = pT_rot.next()
                    trk.op(ACT, lambda: nc.scalar.activation(out=pT[:, pi, 0:W], in_=banks[sc][:, 0:W], func=AF.Exp),
                           reads=[bank_b[sc]], writes=[B_pT[pi]])
                    mi = 2 if sample else (1 if (first and bp == 0) else 0)
                    trk.op(DVE, lambda: nc.vector.tensor_tensor(out=pT[:, pi, 0:W], in0=pT[:, pi, 0:W],
                                                                in1=cstb[:, C_MASK + mi * 512:C_MASK + mi * 512 + W],
                                                                op=ALU.mult),
                           reads=[B_pT[pi], B_cst], writes=[B_pT[pi]])
                    pis.append(pi)
                return pis

            def pv(ci, b, pis, g=g):
                b2 = b % 2
                for (obank, is_den) in ((AV, False), (DEN, True)):
                    for e in range(2):
                        pi = pis[e]
                        po = slice(64 * e, 64 * e + 64)
                        tp_ = (0, 64) if e == 1 else None
                        if sample:
                            lh = ones64[:, :] if is_den else vbuf[:, 1, g * 64:(g + 1) * 64]
                            trk.op(PE, lambda lh=lh: nc.tensor.matmul(
                                banks[obank][po, 0:128], lhsT=lh, rhs=pT[:, pi, 128:256],
                                start=True, stop=False, tile_position=tp_),
                                reads=[B_pT[pi], B_v[1], B_cst], writes=[bank_b[obank]])
                            for bt in range(DEC_B):
                                lh = ones64[:, :] if is_den else vhist[:, bt, g * 64:(g + 1) * 64]
                                trk.op(PE, lambda lh=lh, bt=bt: nc.tensor.matmul(
                                    banks[obank][po, bt * 8:bt * 8 + 8], lhsT=lh,
                                    rhs=pT[:, pi, bt * 8:bt * 8 + 8],
                                    start=False, stop=(bt == DEC_B - 1), tile_position=tp_),
                                    reads=[B_pT[pi], B_vh, B_cst], writes=[bank_b[obank]])
                        else:
                            for kb in range(2):
                                lh = ones64[:, :] if is_den else vbuf[:, b + kb, g * 64:(g + 1) * 64]
                                trk.op(PE, lambda lh=lh, kb=kb: nc.tensor.matmul(
                                    banks[obank][po, b * 128:(b + 1) * 128], lhsT=lh,
                                    rhs=pT[:, pi, (b2 * 2 + kb) * 128:(b2 * 2 + kb + 1) * 128],
                                    start=(kb == 0), stop=(kb == 1), tile_position=tp_),
                                    reads=[B_pT[pi], B_v[b + kb], B_cst], writes=[bank_b[obank]])

            def att_scores(ci):
                npair = 1 if sample else nb // 2
                return [scores(ci, bp) for bp in range(npair)]

            def att_pv(ci, pp, g=g):
                c = 4 * g + ci
                if sample:
                    pv(ci, 0, pp[0])
                else:
                    for b in range(nb):
                        pv(ci, b, pp[b // 2])
                ti_ = tmp_rot.next()
                trk.op(DVE, lambda: nc.vector.tensor_scalar(out=tmpf[:, ti_, 0:T], in0=banks[DEN][:, 0:T],
                                                            scalar1=esink[:, c:c + 1], scalar2=None, op0=ALU.add),
                       reads=[bank_b[DEN], B_cst], writes=[B_tmp[ti_]])
                trk.op(DVE, lambda: nc.vector.reciprocal(out=tmpf[:, ti_, 0:T], in_=tmpf[:, ti_, 0:T]),
                       reads=[B_tmp[ti_]], writes=[B_tmp[ti_]])
                trk.op(DVE, lambda: nc.vector.tensor_tensor(out=attn[:, ci, 0:T], in0=banks[AV][:, 0:T],
                                                            in1=tmpf[:, ti_, 0:T], op=ALU.mult),
                       reads=[bank_b[AV], B_tmp[ti_]], writes=[B_attn[ci]])

            gaw = {}

            def ga_chunk(ci, g=g):
                if ci % 2 == 0:
                    gaw["w"] = w_next(f"ga{g}_{ci // 2}")
                wv, wb = gaw["w"]
                cl = ci % 2
                bk = mm_rot.next()
                for kc in range(NCH):
                    trk.op(PE, lambda kc=kc: nc.tensor.matmul(banks[bk][:, 0:T], lhsT=wv[:, kc, cl * 128:(cl + 1) * 128],
                                                              rhs=hT[:, kc, :], start=(kc == 0), stop=(kc == NCH - 1)),
                           reads=[B_h[kc], wb], writes=[bank_b[bk]])
                si_ = sga_rot.next()
                trk.op(ACT, lambda: nc.scalar.activation(out=sg[:, si_, 0:T], in_=banks[bk][:, 0:T], func=AF.Sigmoid),
                       reads=[bank_b[bk]], writes=[B_sg[si_]] + SG_ALIAS[si_])
                trk.op(DVE, lambda: nc.vector.tensor_tensor(out=attn[:, ci, 0:T], in0=attn[:, ci, 0:T],
                                                            in1=sg[:, si_, 0:T], op=ALU.mult),
                       reads=[B_attn[ci], B_sg[si_]], writes=[B_attn[ci]])
                if ci % 2 == 1:
                    w_done()

            q_gemm(0)
            q_gemm(1)
            ckpt("qg")
            p0_ = att_scores(0)
            q_gemm(2)
            att_pv(0, p0_)
            ckpt("att0")
            p1_ = att_scores(1)
            q_gemm(3)
            att_pv(1, p1_)
            p2_ = att_scores(2)
            ga_chunk(0)
            att_pv(2, p2_)
            p3_ = att_scores(3)
            ga_chunk(1)
            att_pv(3, p3_)
            ckpt("g_q")
            ga_chunk(2)
            ga_chunk(3)
            ckpt("g_ga")
            for ci in range(4):
                c = 4 * g + ci
                if ci % 2 == 0:
                    wv, wb = w_next(f"gp{g}_{ci // 2}")
                cl = ci % 2
                bk = mm_rot.next()
                for kc in range(NCH):
                    trk.op(PE, lambda kc=kc: nc.tensor.matmul(banks[bk][:, 0:T], lhsT=wv[:, kc, cl * 128:(cl + 1) * 128],
                                                              rhs=hT[:, kc, :], start=(kc == 0), stop=(kc == NCH - 1)),
                           reads=[B_h[kc], wb], writes=[bank_b[bk]])
                si_ = sgp_rot.next()
                trk.op(ACT, lambda: nc.scalar.activation(out=sg[:, si_, 0:T], in_=banks[bk][:, 0:T], func=AF.Sigmoid),
                       reads=[bank_b[bk]], writes=[B_sg[si_]] + SG_ALIAS[si_])
                bk2 = mm_rot.next()
                for k2 in range(2):
                    trk.op(PE, lambda k2=k2: nc.tensor.matmul(banks[bk2][:, 0:T],
                                                              lhsT=wpool_sb[:, g, k2, ci * 128:(ci + 1) * 128],
                                                              rhs=pooledT[:, 2 * g + k2, :], start=(k2 == 0), stop=(k2 == 1)),
                           reads=[B_pl[2 * g + k2], B_cst], writes=[bank_b[bk2]])
                ti_ = tmp_rot.next()
                trk.op(DVE, lambda: nc.vector.scalar_tensor_tensor(out=tmpf[:, ti_, 0:T], in0=banks[bk2][:, 0:T],
                                                                   scalar=gcol(G_PSC, c), in1=sg[:, si_, 0:T],
                                                                   op0=ALU.mult, op1=ALU.mult),
                       reads=[bank_b[bk2], B_sg[si_], B_cst], writes=[B_tmp[ti_]])
                trk.op(DVE, lambda: nc.vector.tensor_tensor(out=mixT[:, c, :], in0=attn[:, ci, 0:T],
                                                            in1=tmpf[:, ti_, 0:T], op=ALU.add),
                       reads=[B_attn[ci], B_tmp[ti_]], writes=[B_mix[c]])
                if ci % 2 == 1:
                    w_done()

        ckpt("p2c")
        if not sample and not last:
            trk.op(DVE, lambda: nc.vector.tensor_copy(out=kprev[:], in_=kT2[:, :, 512:640]),
                   reads=[B_kT2], writes=[B_prev])
            trk.op(DVE, lambda: nc.vector.tensor_copy(out=vprev[:], in_=vbuf[:, 4, :]), reads=[B_v[4]],
                   writes=[B_prev])
            trk.op(DVE, lambda: nc.vector.tensor_copy(out=uprev[:], in_=ubuf[:, 4, :]), reads=[B_u[4]],
                   writes=[B_prev])

        pend = []
        for j in range(8):
            wv, wb = w_next(f"o{j}")
            for ci in range(2):
                c = 2 * j + ci
                bk = big_rot.next()
                for kc in range(NCH):
                    trk.op(PE, lambda kc=kc: nc.tensor.matmul(banks[bk][:, 0:T], lhsT=wv[:, kc, ci * 128:(ci + 1) * 128],
                                                              rhs=mixT[:, kc, :], start=(kc == 0), stop=(kc == NCH - 1)),
                           reads=[B_mix[kc], wb], writes=[bank_b[bk]])
                si = sq_rot.next()
                trk.op(DVE, lambda: nc.vector.tensor_copy(out=fT[:, c, :], in_=banks[bk][:, 0:T]),
                       reads=[bank_b[bk]], writes=[B_f[c]])
                trk.op(ACT, lambda: nc.scalar.activation(out=sq[:, si, 0:T], in_=fT[:, c, :], func=AF.Square),
                       reads=[B_f[c]], writes=[B_sq[si]])
                pend.append((si, c))
                if len(pend) > 1:
                    stat_mm(*pend.pop(0))
            w_done()
        nxt = NEXT_PROMPT.get(ti) if (not sample and PREFETCH_X) else None
        if nxt is not None:
            for tb in range(2):
                trk.dma(SP, mixF[:, tb * 2048:(tb + 1) * 2048], xp[nxt * 512 + tb * 128: nxt * 512 + (tb + 1) * 128, :],
                        xsem[tb], writes=B_mix[8 * tb:8 * tb + 8])
        while pend:
            stat_mm(*pend.pop(0))
        stats_rinv(1)
        pend = []
        for c in range(NCH):
            ti_ = tmp_rot.next()
            trk.op(DVE, lambda: nc.vector.scalar_tensor_tensor(out=tmpf[:, ti_, 0:T], in0=fT[:, c, :], scalar=gcol(G_POST, c),
                                                               in1=rinv[:, 1, 0:T], op0=ALU.mult, op1=ALU.mult),
                   reads=[B_f[c], B_rinv[1], B_cst], writes=[B_tmp[ti_]])
            trk.op(DVE, lambda: nc.vector.tensor_tensor(out=xT[:, c, 0:T], in0=xT[:, c, 0:T], in1=tmpf[:, ti_, 0:T],
                                                        op=ALU.add),
                   reads=[B_xT[c], B_tmp[ti_]], writes=[B_xT[c]])
            trk.op(ACT, lambda: nc.scalar.activation(out=hT[:, c, :], in_=xT[:, c, 0:T], func=AF.Copy,
                                                     scale=gcol(G_MPRE, c)),
                   reads=[B_xT[c], B_cst], writes=[B_h[c]])
        r3sq = attn[:, 0, 0:T]
        r3q = attn[:, 1, 0:T]

        def x1_stat(c):
            pi = pT_rot.next()
            trk.op(ACT, lambda: nc.scalar.activation(out=pT_sb[:, pi, 0:T], in_=xT[:, c, 0:T], func=AF.Square),
                   reads=[B_xT[c]], writes=[B_pT[pi]])
            return (pi, c)

        def x1_stat_mm(pi, c):
            trk.op(PE, lambda: nc.tensor.matmul(banks[TR][:, 0:T], lhsT=onesD[:], rhs=pT_sb[:, pi, 0:T],
                                                start=(c == 0), stop=(c == NCH - 1)),
                   reads=[B_pT[pi], B_cst], writes=[bank_b[TR]])

        def x1_rinv():
            trk.op(DVE, lambda: nc.vector.tensor_scalar(out=r3sq, in0=banks[TR][:, 0:T], scalar1=EPS, scalar2=None,
                                                        op0=ALU.add), reads=[bank_b[TR]], writes=[B_attn[0]])
            trk.op(DVE, lambda: nc.vector.reciprocal(out=r3sq, in_=r3sq), reads=[B_attn[0]], writes=[B_attn[0]])
            trk.op(DVE, lambda: nc.vector.tensor_tensor(out=r3q, in0=r3sq, in1=r3sq, op=ALU.mult), reads=[B_attn[0]],
                   writes=[B_attn[1]])

        ckpt("p3")
        pend = []
        for half in range(2):
            for j in range(16):
                if half == 0:
                    pend.append(x1_stat(j))
                    if len(pend) > 4:
                        x1_stat_mm(*pend.pop(0))
                wv, wb = w_next(f"up{half * 16 + j}")
                for ci in range(2):
                    uc = 2 * j + ci
                    bk = big_rot.next()
                    for kc in range(NCH):
                        trk.op(PE, lambda kc=kc: nc.tensor.matmul(banks[bk][:, 0:T],
                                                                  lhsT=wv[:, kc, ci * 128:(ci + 1) * 128],
                                                                  rhs=hT[:, kc, :], start=(kc == 0), stop=(kc == NCH - 1)),
                               reads=[B_h[kc], wb], writes=[bank_b[bk]])
                    ti_ = tmp_rot.next()
                    trk.op(ACT, lambda: nc.scalar.activation(out=tmpf[:, ti_, 0:T], in_=banks[bk][:, 0:T], func=AF.Relu),
                           reads=[bank_b[bk]], writes=[B_tmp[ti_]])
                    trk.op(DVE, lambda: nc.vector.tensor_tensor(out=upT[:, uc, :], in0=tmpf[:, ti_, 0:T],
                                                                in1=tmpf[:, ti_, 0:T], op=ALU.mult),
                           reads=[B_tmp[ti_]], writes=[B_up[uc]])
                w_done()
            if half == 0:
                while pend:
                    x1_stat_mm(*pend.pop(0))
                x1_rinv()
            if half == 1 and nxt is not None:
                for tb in range(2, 4):
                    trk.dma(SP, hF[:, (tb - 2) * 2048:(tb - 1) * 2048],
                            xp[nxt * 512 + tb * 128: nxt * 512 + (tb + 1) * 128, :], xsem[tb],
                            writes=B_h[8 * (tb - 2):8 * (tb - 2) + 8])
                pf_state["tile"] = nxt
            for cp in range(8):
                bks = (big_rot.next(), big_rot.next())
                for kq in range(2):
                    wv, wb = w_next(f"dn{half}_{cp}_{kq}")
                    for ci in range(2):
                        bk = bks[ci]
                        for kc in range(16):
                            kk = kq * 16 + kc
                            trk.op(PE, lambda kc=kc, kk=kk: nc.tensor.matmul(
                                banks[bk][:, 0:T], lhsT=wv[:, kc, ci * 128:(ci + 1) * 128], rhs=upT[:, kk, :],
                                start=(kq == 0 and kc == 0), stop=(kq == 1 and kc == 15)),
                                reads=[B_up[kk], wb], writes=[bank_b[bk]])
                    w_done()
                for ci in range(2):
                    c = 2 * cp + ci
                    bk = bks[ci]
                    if half == 0:
                        trk.op(DVE, lambda: nc.vector.tensor_copy(out=fT[:, c, :], in_=banks[bk][:, 0:T]),
                               reads=[bank_b[bk]], writes=[B_f[c]])
                    else:
                        trk.op(DVE, lambda: nc.vector.tensor_tensor(out=fT[:, c, :], in0=banks[bk][:, 0:T],
                                                                    in1=fT[:, c, :], op=ALU.add),
                               reads=[bank_b[bk], B_f[c]], writes=[B_f[c]])
                        si = sq_rot.next()
                        trk.op(ACT, lambda: nc.scalar.activation(out=sq[:, si, 0:T], in_=fT[:, c, :], func=AF.Square),
                               reads=[B_f[c]], writes=[B_sq[si]])
                        pend.append((si, c))
                        if len(pend) > 1:
                            stat_mm(*pend.pop(0))
        while pend:
            stat_mm(*pend.pop(0))
        r4 = rinv[:, 1, 0:T]
        trk.op(DVE, lambda: nc.vector.tensor_tensor(out=r4, in0=banks[TR][:, 0:T], in1=r3q, op=ALU.mult),
               reads=[bank_b[TR], B_attn[1]], writes=[B_rinv[1]])
        trk.op(DVE, lambda: nc.vector.tensor_scalar(out=r4, in0=r4, scalar1=EPS, scalar2=None, op0=ALU.add),
               reads=[B_rinv[1]], writes=[B_rinv[1]])
        trk.op(ACT, lambda: nc.scalar.activation(out=r4, in_=r4, func=AF.Sqrt), reads=[B_rinv[1]], writes=[B_rinv[1]])
        trk.op(DVE, lambda: nc.vector.reciprocal(out=r4, in_=r4), reads=[B_rinv[1]], writes=[B_rinv[1]])
        trk.op(DVE, lambda: nc.vector.tensor_tensor(out=r4, in0=r4, in1=r3sq, op=ALU.mult),
               reads=[B_rinv[1], B_attn[0]], writes=[B_rinv[1]])
        for c in range(NCH):
            ti_ = tmp_rot.next()
            trk.op(DVE, lambda: nc.vector.scalar_tensor_tensor(out=tmpf[:, ti_, 0:T], in0=fT[:, c, :], scalar=gcol(G_MPOST, c),
                                                               in1=rinv[:, 1, 0:T], op0=ALU.mult, op1=ALU.mult),
                   reads=[B_f[c], B_rinv[1], B_cst], writes=[B_tmp[ti_]])
            trk.op(DVE, lambda: nc.vector.tensor_tensor(out=xT[:, c, 0:T], in0=xT[:, c, 0:T], in1=tmpf[:, ti_, 0:T],
                                                        op=ALU.add),
                   reads=[B_xT[c], B_tmp[ti_]], writes=[B_xT[c]])
        ckpt("p5")
        for tb in range(nb):
            for cg in range(4):
                bk = big_rot.next()
                for ci in range(4):
                    c = 4 * cg + ci
                    trk.op(PE, lambda ci=ci, c=c: nc.tensor.transpose(banks[bk][:, ci * 128:(ci + 1) * 128],
                                                                      xT[:, c, tb * 128:(tb + 1) * 128], identF[:]),
                           reads=[B_xT[c], B_cst], writes=[bank_b[bk]])
                evac_copy(ev_rot.next(), fbuf[:, tb * 2048 + cg * 512: tb * 2048 + (cg + 1) * 512], banks[bk][:],
                          [bank_b[bk]], [B_f[4 * tb + cg]])
            dst = ys if sample else yp[ti * 512 + tb * 128: ti * 512 + (tb + 1) * 128, :]
            trk.dma(SP, dst, fbuf[:, tb * 2048:(tb + 1) * 2048], ysem[tb], reads=B_f[4 * tb:4 * tb + 4], writes=[B_y])

    try:
        if "p0" in TILES:
            run_tile("p", 0)
        if "p1" in TILES:
            run_tile("p", 1)
        if "s" in TILES:
            trk.barrier()
            run_tile("s", 0)
            trk.barrier()
        for ti in range(2, 4):
            if f"p{ti}" in TILES:
                run_tile("p", ti)
        assert wstate["cons"] == total_blocks, (wstate, total_blocks)
    except StopBuild:
        pass
    for _ in range(PAD_DMA):
        trk.dma(POOL, wpool_sb[:], w_pool.rearrange("g (kc p) n -> p g kc n", p=128), csem_p, writes=[B_cst])
    for _ in range(PAD_PE):
        trk.op(PE, lambda: nc.tensor.matmul(banks[0][:, 0:8], lhsT=onesD[:], rhs=onesD[:, 0:8], start=True, stop=True),
               reads=[B_cst], writes=[bank_b[0]])
    trk.barrier()
    return nc


_CACHE = {}


def _get_program():
    if "nc" not in _CACHE:
        _CACHE["nc"] = build_program()
        _CACHE["cst"] = _build_consts()
    return _CACHE["nc"], _CACHE["cst"]


def kernel(x_prompt, x_sample, cache_k_win, cache_v_win, state_pool, norm_attn_pre, norm_attn_post, w_in,
           attn_sinks, w_pool, pool_scale, w_out, norm_mlp_pre, norm_mlp_post, w_up, w_down):
    nc, cst = _get_program()
    f = lambda a: np.ascontiguousarray(np.asarray(a, dtype=np.float32))
    x_prompt = f(x_prompt); x_sample = f(x_sample)
    ck = f(cache_k_win)[0].reshape(128, 128, 256)
    cv = f(cache_v_win)[0].reshape(128, 128, 256)
    sp = f(state_pool)[0]
    gl = [f(norm_attn_pre)[0], f(norm_attn_post)[0], f(pool_scale)[0], f(norm_mlp_pre)[0], f(norm_mlp_post)[0]]
    gvec = np.ascontiguousarray(np.stack([g.reshape(16, 128).T for g in gl], axis=1).reshape(128, 80))
    sk = f(attn_sinks)[0].reshape(16, 2)
    sinks = np.ascontiguousarray(np.repeat(sk.T, 64, axis=0))
    shared = {"w_in": f(w_in)[0], "w_pool": f(w_pool)[0], "w_out": f(w_out)[0], "w_up": f(w_up)[0],
              "w_down": f(w_down)[0], "gvec": gvec, "sinks": sinks, "cst": cst}
    in_maps = []
    for i in range(N_CORES):
        m = dict(shared)
        m["xp"] = x_prompt[i]
        m["xs"] = np.ascontiguousarray(x_sample[16 * i:16 * (i + 1)].reshape(128, D))
        m["ck"] = np.ascontiguousarray(ck[16 * i:16 * (i + 1)])
        m["cv"] = np.ascontiguousarray(cv[16 * i:16 * (i + 1)])
        m["spool"] = np.ascontiguousarray(sp[16 * i:16 * (i + 1)])
        in_maps.append(m)
    res = run_bass_kernel_spmd(nc, in_maps, core_ids=list(range(N_CORES)))
    r = res.results
    y_prompt = np.stack([r[i]["yp"] for i in range(N_CORES)], axis=0)
    y_sample = np.concatenate([r[i]["ys"].reshape(16, 8, D) for i in range(N_CORES)], axis=0)
    k_win_prompt = np.stack([r[i]["kp"].reshape(128, 4, 64) for i in range(N_CORES)], axis=0)[None]
    v_win_prompt = np.stack([r[i]["vp"].reshape(128, 4, 64) for i in range(N_CORES)], axis=0)[None]
    pool_prompt = np.stack([r[i]["upo"] for i in range(N_CORES)], axis=0)[None]
    k_win_sample = np.concatenate([r[i]["kso"].reshape(16, 128, 4, 64) for i in range(N_CORES)], axis=0)[None]
    v_win_sample = np.concatenate([r[i]["vso"].reshape(16, 128, 4, 64) for i in range(N_CORES)], axis=0)[None]
    pool_sample = np.concatenate([r[i]["uso"] for i in range(N_CORES)], axis=0)[None]
    return (y_prompt.astype(np.float32), y_sample.astype(np.float32), k_win_prompt.astype(np.float32),
            v_win_prompt.astype(np.float32), pool_prompt.astype(np.float32), k_win_sample.astype(np.float32),
            v_win_sample.astype(np.float32), pool_sample.astype(np.float32))
```
